# Optimizing a Trainium2 kernel written in Bass

```python
import jax, jax.numpy as jnp
from jax import lax
import numpy as np

D_MODEL = 1024
BATCH = 8
SEQ = 2048
DEPTH = 4

CTX_LEN = 256
GRID_W = 64

N_HEADS = 8
Q_LORA = 384
KV_LORA = 256
QK_NOPE = 64
QK_ROPE = 32
V_DIM = 64
QK_DIM = QK_NOPE + QK_ROPE
ROPE_BASE = 10000.0
Q_BLOCK = 128
F_GROUPS = 4
F_GROUP_W = 128
F_WIDTH = F_GROUPS * F_GROUP_W
RNN_WIDTH = 512
RNN_HEADS = 8
RNN_BLOCK = RNN_WIDTH // RNN_HEADS
RNN_CONV = 4
LRU_C = 8.0
D_FF = 2816
FFN_CONV = 3
N_BRANCH = 3
N_MOD = 6
EPS = 1e-6

IN_SIZES = (Q_LORA, KV_LORA, QK_ROPE, F_WIDTH, RNN_WIDTH, RNN_WIDTH, N_BRANCH * D_MODEL)
D_IN = Q_LORA + KV_LORA + QK_ROPE + F_WIDTH + 2 * RNN_WIDTH + N_BRANCH * D_MODEL

kernel_name = 'hybrid_mla_fnet_rglru_prefix_dit'


def rms_norm(x, g):
    xf = x.astype(jnp.float32)
    y = xf * lax.rsqrt(jnp.mean(xf * xf, axis=-1, keepdims=True) + EPS)
    return (y * g.astype(jnp.float32)).astype(x.dtype)


def modulate(h, shift, scale):
    return h * (1 + scale) + shift


def split_in(p):
    idx = np.cumsum(IN_SIZES)[:-1].tolist()
    return jnp.split(p, idx, axis=-1)


def axial_rope(n_rows, dtype):
    row = jnp.repeat(jnp.arange(n_rows, dtype=jnp.float32), GRID_W)
    col = jnp.tile(jnp.arange(GRID_W, dtype=jnp.float32), n_rows)
    n_freq = QK_ROPE // 4
    inv = ROPE_BASE ** (-jnp.arange(n_freq, dtype=jnp.float32) / n_freq)
    ang = jnp.concatenate([row[:, None] * inv, col[:, None] * inv], axis=-1)
    return jnp.cos(ang).astype(dtype), jnp.sin(ang).astype(dtype)


def apply_rope(x, cos, sin):
    half = x.shape[-1] // 2
    x1, x2 = x[..., :half], x[..., half:]
    return jnp.concatenate([x1 * cos - x2 * sin, x2 * cos + x1 * sin], axis=-1)


def mla_q(cq, q_norm, w_uq, rope):
    q = jnp.einsum('bld,dhe->blhe', rms_norm(cq, q_norm), w_uq)
    q_nope, q_rope = q[..., :QK_NOPE], q[..., QK_NOPE:]
    if rope is not None:
        q_rope = apply_rope(q_rope, rope[0][:, None, :], rope[1][:, None, :])
    return jnp.concatenate([q_nope, q_rope], axis=-1)


def mla_kv(ckv, kr, kv_norm, w_ukv, rope):
    kv = jnp.einsum('bld,dhe->blhe', rms_norm(ckv, kv_norm), w_ukv)
    k_nope, v = kv[..., :QK_NOPE], kv[..., QK_NOPE:]
    if rope is not None:
        kr = apply_rope(kr, rope[0], rope[1])
    k_rope = jnp.broadcast_to(kr[:, :, None, :], k_nope.shape[:-1] + (QK_ROPE,))
    return jnp.concatenate([k_nope, k_rope], axis=-1), v


def attend(q, k, v):
    s = jnp.einsum('bqhe,bkhe->bhqk', q, k).astype(jnp.float32) * (QK_DIM ** -0.5)
    p = jax.nn.softmax(s, axis=-1).astype(v.dtype)
    return jnp.einsum('bhqk,bkhv->bqhv', p, v)


def blocked_attention(q, k, v):
    B, L, H, E = q.shape
    nb = L // Q_BLOCK
    qb = q.reshape(B, nb, Q_BLOCK, H, E).transpose(1, 0, 2, 3, 4)
    out = lax.map(lambda qi: attend(qi, k, v), qb)
    return out.transpose(1, 0, 2, 3, 4).reshape(B, L, H, v.shape[-1])


def fourier_mix(u):
    B, L, _ = u.shape
    ug = u.reshape(B, L, F_GROUPS, F_GROUP_W).astype(jnp.float32)
    y = jnp.fft.fftn(ug, axes=(1, 3), norm='ortho').real
    return y.reshape(B, L, F_WIDTH).astype(u.dtype)


def depthwise_conv(x, w, b, pad_left):
    K = w.shape[0]
    L = x.shape[1]
    xp = jnp.pad(x, ((0, 0), (pad_left, K - 1 - pad_left), (0, 0)))
    y = b + w[0] * xp[:, 0:L]
    for k in range(1, K):
        y = y + w[k] * xp[:, k:k + L]
    return y


def _lin_combine(left, right):
    a1, b1 = left
    a2, b2 = right
    return a1 * a2, a2 * b1 + b2


def rglru_scan(u, params, h0, reverse):
    w_a, b_a, w_x, b_x, lam = params
    B, L, R = u.shape
    ub = u.reshape(B, L, RNN_HEADS, RNN_BLOCK)
    r = jax.nn.sigmoid((jnp.einsum('blhi,hij->blhj', ub, w_a).reshape(B, L, R) + b_a).astype(jnp.float32))
    gi = jax.nn.sigmoid((jnp.einsum('blhi,hij->blhj', ub, w_x).reshape(B, L, R) + b_x).astype(jnp.float32))
    log_a = -LRU_C * r * jax.nn.softplus(-lam.astype(jnp.float32))
    a = jnp.exp(log_a)
    b = jnp.sqrt(-jnp.expm1(2.0 * log_a)) * gi * u.astype(jnp.float32)
    A, Bc = lax.associative_scan(_lin_combine, (a, b), axis=1, reverse=reverse)
    return A * h0[:, None, :] + Bc


def merge(y_att, y_f, y_rnn, gate_logits, w_out):
    g = jax.nn.sigmoid(gate_logits.astype(jnp.float32)).astype(y_att.dtype)
    g_att, g_f, g_rnn = jnp.split(g, N_BRANCH, axis=-1)
    return (g_att * y_att + g_f * y_f + g_rnn * y_rnn) @ w_out


def conv_ffn(h, w_up, cw, cb, w_down):
    u = depthwise_conv(h @ w_up, cw, cb, FFN_CONV // 2)
    up, gate = jnp.split(u, 2, axis=-1)
    return (jax.nn.gelu(gate) * up) @ w_down


def setup_inputs(seed: int = 0) -> dict:
    key = jax.random.key(seed)
    ks = jax.random.split(key, 40)
    f32 = jnp.float32
    L, D = DEPTH, D_MODEL

    def nrm(k, shape, scale):
        return jax.random.normal(k, shape, f32) * scale

    def gain(k, shape):
        return 1.0 + 0.02 * jax.random.normal(k, shape, f32)

    u = jax.random.uniform(ks[20], (L, 2, RNN_WIDTH), f32, 0.9, 0.999)
    s = u ** (1.0 / LRU_C)
    rg_lambda = jnp.log(s) - jnp.log1p(-s)
    return {
        'x': nrm(ks[0], (BATCH, SEQ, D), 1.0),
        'c': nrm(ks[1], (BATCH, D), 1.0),
        'ctx': nrm(ks[2], (BATCH, CTX_LEN, D), 1.0),
        'c_ctx': nrm(ks[3], (D,), 1.0),
        'w_ada': nrm(ks[4], (L, D, N_MOD * D), 0.02),
        'b_ada': nrm(ks[5], (L, N_MOD * D), 0.02),
        'norm_mix': gain(ks[6], (L, D)),
        'norm_ffn': gain(ks[7], (L, D)),
        'w_in': nrm(ks[8], (L, D, D_IN), D ** -0.5),
        'q_norm': gain(ks[9], (L, Q_LORA)),
        'kv_norm': gain(ks[10], (L, KV_LORA)),
        'w_uq': nrm(ks[11], (L, Q_LORA, N_HEADS, QK_DIM), Q_LORA ** -0.5),
        'w_ukv': nrm(ks[12], (L, KV_LORA, N_HEADS, QK_NOPE + V_DIM), KV_LORA ** -0.5),
        'w_o_attn': nrm(ks[13], (L, N_HEADS * V_DIM, D), (N_HEADS * V_DIM) ** -0.5),
        'w_o_fourier': nrm(ks[14], (L, F_WIDTH, D), F_WIDTH ** -0.5),
        'rnn_conv_w': nrm(ks[15], (L, RNN_CONV, RNN_WIDTH), RNN_CONV ** -0.5),
        'rnn_conv_b': nrm(ks[16], (L, RNN_WIDTH), 0.02),
        'rg_w_a': nrm(ks[17], (L, 2, RNN_HEADS, RNN_BLOCK, RNN_BLOCK), RNN_BLOCK ** -0.5),
        'rg_b_a': nrm(ks[18], (L, 2, RNN_WIDTH), 0.02),
        'rg_w_x': nrm(ks[19], (L, 2, RNN_HEADS, RNN_BLOCK, RNN_BLOCK), RNN_BLOCK ** -0.5),
        'rg_b_x': nrm(ks[21], (L, 2, RNN_WIDTH), 0.02),
        'rg_lambda': rg_lambda,
        'w_o_rnn': nrm(ks[22], (L, RNN_WIDTH, D), RNN_WIDTH ** -0.5),
        'w_out': nrm(ks[23], (L, D, D), D ** -0.5),
        'w_up': nrm(ks[24], (L, D, 2 * D_FF), D ** -0.5),
        'ffn_conv_w': nrm(ks[25], (L, FFN_CONV, 2 * D_FF), FFN_CONV ** -0.5),
        'ffn_conv_b': nrm(ks[26], (L, 2 * D_FF), 0.02),
        'w_down': nrm(ks[27], (L, D_FF, D), D_FF ** -0.5),
        'final_norm': gain(ks[28], (D,)),
    }


def reference(x, c, ctx, c_ctx, w_ada, b_ada, norm_mix, norm_ffn, w_in, q_norm, kv_norm, w_uq, w_ukv,
              w_o_attn, w_o_fourier, rnn_conv_w, rnn_conv_b, rg_w_a, rg_b_a, rg_w_x, rg_b_x, rg_lambda,
              w_o_rnn, w_out, w_up, ffn_conv_w, ffn_conv_b, w_down, final_norm):
    B, S, _ = x.shape
    C = ctx.shape[1]
    dt = x.dtype
    n_rows = S // GRID_W
    rope = axial_rope(n_rows, dt)
    silu_c = jax.nn.silu(c)
    silu_cc = jax.nn.silu(c_ctx)
    h0_zero = jnp.zeros((B, RNN_WIDTH), jnp.float32)
    for l in range(DEPTH):
        last = l == DEPTH - 1
        mx = jnp.split((silu_c @ w_ada[l] + b_ada[l])[:, None, :], N_MOD, axis=-1)
        mc = jnp.split(silu_cc @ w_ada[l] + b_ada[l], N_MOD, axis=-1)

        hx = modulate(rms_norm(x, norm_mix[l]), mx[0], mx[1])
        hc = modulate(rms_norm(ctx, norm_mix[l]), mc[0], mc[1])
        cq_x, ckv_x, kr_x, uf_x, ur_x, ug_x, gl_x = split_in(hx @ w_in[l])
        cq_c, ckv_c, kr_c, uf_c, ur_c, ug_c, gl_c = split_in(hc @ w_in[l])

        k_c, v_c = mla_kv(ckv_c, kr_c, kv_norm[l], w_ukv[l], None)
        k_x, v_x = mla_kv(ckv_x, kr_x, kv_norm[l], w_ukv[l], rope)
        q_x = mla_q(cq_x, q_norm[l], w_uq[l], rope)
        att_x = blocked_attention(q_x, jnp.concatenate([k_c, k_x], axis=1), jnp.concatenate([v_c, v_x], axis=1))

        dir_f = (rg_w_a[l, 0], rg_b_a[l, 0], rg_w_x[l, 0], rg_b_x[l, 0], rg_lambda[l, 0])
        dir_b = (rg_w_a[l, 1], rg_b_a[l, 1], rg_w_x[l, 1], rg_b_x[l, 1], rg_lambda[l, 1])
        uc = depthwise_conv(ur_c, rnn_conv_w[l], rnn_conv_b[l], RNN_CONV // 2)
        h_cf = rglru_scan(uc, dir_f, h0_zero, False)
        h_cb = rglru_scan(uc, dir_b, h0_zero, True)
        ux = depthwise_conv(ur_x, rnn_conv_w[l], rnn_conv_b[l], RNN_CONV // 2)
        h_xf = rglru_scan(ux, dir_f, h_cf[:, -1], False)
        h_xb = rglru_scan(ux, dir_b, h_cb[:, 0], True)

        y_x = merge(att_x.reshape(B, S, N_HEADS * V_DIM) @ w_o_attn[l],
                    fourier_mix(uf_x) @ w_o_fourier[l],
                    ((h_xf + h_xb).astype(dt) * jax.nn.gelu(ug_x)) @ w_o_rnn[l],
                    gl_x, w_out[l])
        x = x + mx[2] * y_x
        x = x + mx[5] * conv_ffn(modulate(rms_norm(x, norm_ffn[l]), mx[3], mx[4]),
                                 w_up[l], ffn_conv_w[l], ffn_conv_b[l], w_down[l])

        if not last:
            q_c = mla_q(cq_c, q_norm[l], w_uq[l], None)
            att_c = attend(q_c, k_c, v_c)
            y_c = merge(att_c.reshape(B, C, N_HEADS * V_DIM) @ w_o_attn[l],
                        fourier_mix(uf_c) @ w_o_fourier[l],
                        ((h_cf + h_cb).astype(dt) * jax.nn.gelu(ug_c)) @ w_o_rnn[l],
                        gl_c, w_out[l])
            ctx = ctx + mc[2] * y_c
            ctx = ctx + mc[5] * conv_ffn(modulate(rms_norm(ctx, norm_ffn[l]), mc[3], mc[4]),
                                         w_up[l], ffn_conv_w[l], ffn_conv_b[l], w_down[l])
    return rms_norm(x, final_norm)
```

```python
import os
import math
import numpy as np
import ml_dtypes
import concourse.bass as bass
import concourse.mybir as mybir
from concourse.bass_utils import run_bass_kernel_spmd

F32 = mybir.dt.float32
BF16 = mybir.dt.bfloat16
AF = mybir.ActivationFunctionType
ALU = mybir.AluOpType

L = 4
D = 1024
KD = 8
S = 2048
C = 256
NT = S + C
QL, KVL, ROPE = 384, 256, 32
NH = 8
FW = 512
RW = 512
DFF = 2816
NJ = DFF // 128
EPS = 1e-6
TILES = [(0, 256), (256, 768), (768, 1280), (1280, 1792), (1792, 2304)]


class Buf:
    __slots__ = ("name", "w", "r", "psum")

    def __init__(self, name, psum=False):
        self.name = name
        self.w = None
        self.r = {}
        self.psum = psum


class Trk:
    LIMIT = 30000

    def __init__(self, nc, n_dma=12):
        self.nc = nc
        self.eng = {}
        for name, obj in (("pe", nc.tensor), ("act", nc.scalar), ("dve", nc.vector),
                          ("pool", nc.gpsimd), ("sp", nc.sync)):
            self.eng[name] = {"obj": obj, "sem": nc.alloc_semaphore(f"s_{name}_0"), "cnt": 0, "ep": 0,
                              "key": f"{name}_0"}
        self.waited = {name: {} for name in self.eng}
        self.dma = {}
        for q in ("sp", "pool"):
            self.dma[q] = {"slots": [{"sem": nc.alloc_semaphore(f"d_{q}_{i}"), "cum": 0, "key": f"d_{q}_{i}"}
                                     for i in range(n_dma)], "next": 0}
        self.n_wait = 0
        self.n_ins = 0

    def _wait(self, eng, dep):
        key, sem, val = dep
        w = self.waited[eng]
        if w.get(key, 0) >= val:
            return
        self.eng[eng]["obj"].wait_ge(sem, val)
        w[key] = val
        self.n_wait += 1

    def _deps(self, reads, writes):
        deps = {}

        def add(d):
            if d is None:
                return
            if d[0] not in deps or deps[d[0]][2] < d[2]:
                deps[d[0]] = d
        for b in reads:
            add(b.w)
            if b.psum:
                for d in b.r.values():
                    add(d)
        for b in writes:
            add(b.w)
            for d in b.r.values():
                add(d)
        return deps.values()

    def _record(self, dep, reads, writes):
        for b in reads:
            b.r[dep[0]] = dep
        for b in writes:
            b.w = dep
            b.r = {}

    def _bump(self, eng):
        E = self.eng[eng]
        if E["cnt"] >= self.LIMIT:
            E["ep"] += 1
            E["sem"] = self.nc.alloc_semaphore(f"s_{eng}_{E['ep']}")
            E["cnt"] = 0
            E["key"] = f"{eng}_{E['ep']}"
        E["cnt"] += 1
        return (E["key"], E["sem"], E["cnt"])

    def op(self, eng, fn, reads=(), writes=()):
        for d in self._deps(reads, writes):
            if eng == "pe" and d[0].startswith("pe_"):
                continue
            self._wait(eng, d)
        ins = fn(self.eng[eng]["obj"])
        dep = self._bump(eng)
        ins.then_inc(dep[1], 1)
        self._record(dep, reads, writes)
        self.n_ins += 1
        return ins

    def mm(self, out_buf, out_ap, pairs, reads):
        for d in self._deps(reads, (out_buf,)):
            if d[0].startswith("pe_"):
                continue
            self._wait("pe", d)
        n = len(pairs)
        pe = self.eng["pe"]["obj"]
        for i, (lt, rh) in enumerate(pairs):
            ins = pe.matmul(out_ap, lhsT=lt, rhs=rh, start=(i == 0), stop=(i == n - 1))
            self.n_ins += 1
        dep = self._bump("pe")
        ins.then_inc(dep[1], 1)
        self._record(dep, reads, (out_buf,))

    def dma_op(self, q, out, in_, reads=(), writes=(), **kw):
        for d in self._deps(reads, writes):
            self._wait(q, d)
        Q = self.dma[q]
        slot = Q["slots"][Q["next"]]
        Q["next"] = (Q["next"] + 1) % len(Q["slots"])
        if slot["cum"] > 0:
            self._wait(q, (slot["key"], slot["sem"], slot["cum"]))
        if slot["cum"] >= self.LIMIT:
            slot["sem"] = self.nc.alloc_semaphore(slot["key"] + "n")
            slot["key"] = slot["key"] + "n"
            slot["cum"] = 0
        ins = self.eng[q]["obj"].dma_start(out=out, in_=in_, **kw)
        slot["cum"] += 16
        ins.then_inc(slot["sem"], 16)
        dep = (slot["key"], slot["sem"], slot["cum"])
        self._record(dep, reads, writes)
        self.n_ins += 1
        return dep

    def barrier(self):
        fence = []
        for name, E in self.eng.items():
            if E["cnt"] > 0:
                fence.append((E["key"], E["sem"], E["cnt"]))
        for q in self.dma.values():
            for s in q["slots"]:
                if s["cum"] > 0:
                    fence.append((s["key"], s["sem"], s["cum"]))
        for name in self.eng:
            for d in fence:
                if d[0].startswith(name + "_"):
                    continue
                self._wait(name, d)

    def wait_all(self, eng):
        for q in self.dma.values():
            for s in q["slots"]:
                if s["cum"] > 0:
                    self._wait(eng, (s["key"], s["sem"], s["cum"]))
        for name, E in self.eng.items():
            if name != eng and E["cnt"] > 0:
                self._wait(eng, (E["key"], E["sem"], E["cnt"]))


def _pk(w):
    K, N = w.shape
    return np.ascontiguousarray(w.reshape(K // 128, 128, N).transpose(1, 0, 2))


def _vec(v):
    sh = v.shape
    return np.ascontiguousarray(np.moveaxis(v.reshape(sh[:-1] + (sh[-1] // 128, 128)), -1, 0))


def _tables():
    t = {}
    n_rows = S // 64
    row = np.repeat(np.arange(n_rows, dtype=np.float32), 64)
    col = np.tile(np.arange(64, dtype=np.float32), n_rows)
    nf = ROPE // 4
    inv = (np.float32(10000.0) ** (-np.arange(nf, dtype=np.float32) / nf)).astype(np.float32)
    ang = np.concatenate([row[:, None] * inv, col[:, None] * inv], axis=-1).astype(np.float32)
    cos = np.cos(ang).astype(np.float32).T
    sin = np.sin(ang).astype(np.float32).T
    t["rope_cos"] = np.ascontiguousarray(np.concatenate([cos, cos], 0))
    t["rope_sin"] = np.ascontiguousarray(np.concatenate([-sin, sin], 0))
    bf = ml_dtypes.bfloat16

    def dft(n):
        k = np.arange(n, dtype=np.int64)
        a = (np.outer(k, k) % n).astype(np.float64) * (2.0 * np.pi / n)
        return np.cos(a), np.sin(a)
    cc, sc = dft(128)
    t["dft_c"] = np.ascontiguousarray(np.concatenate([cc, sc], 1).astype(bf))
    cl, sl = dft(S)
    t["dft_lc"] = np.ascontiguousarray(cl.reshape(S // 128, 128, S).transpose(1, 0, 2).astype(bf))
    t["dft_ls"] = np.ascontiguousarray((-sl).reshape(S // 128, 128, S).transpose(1, 0, 2).astype(bf))
    c2, s2 = dft(C)
    t["dft_cc"] = np.ascontiguousarray(c2.reshape(C // 128, 128, C).transpose(1, 0, 2).astype(bf))
    t["dft_cs"] = np.ascontiguousarray((-s2).reshape(C // 128, 128, C).transpose(1, 0, 2).astype(bf))
    return t


def prep_shared(inp):
    f = np.float32
    sh = {}
    w_in = inp["w_in"]
    o_cq, o_ckv, o_kr, o_uf, o_ur, o_ug, o_gl = 0, 384, 640, 672, 1184, 1696, 2208
    kr_cols = np.arange(o_kr, o_kr + 32)
    kr_sw = np.concatenate([kr_cols[16:], kr_cols[:16]])
    cols_a = np.concatenate([np.arange(0, 672), np.arange(o_ckv + 192, o_ckv + 256), kr_sw])
    sh["w_in_a"] = np.stack([_pk(w_in[l][:, cols_a]) for l in range(L)])
    sh["w_in_f"] = np.stack([_pk(w_in[l][:, o_uf:o_uf + 512]) for l in range(L)])
    sh["w_in_r"] = np.stack([_pk(w_in[l][:, o_ur:o_ur + 512]) for l in range(L)])
    sh["w_in_g"] = np.stack([_pk(w_in[l][:, o_ug:o_ug + 512]) for l in range(L)])
    sh["w_in_gl"] = np.stack([_pk(w_in[l][:, o_gl:o_gl + 3072]) for l in range(L)])
    w_uq = inp["w_uq"]
    sh["w_uq_a"] = np.stack([_pk(w_uq[l].reshape(QL, NH * 96)) for l in range(L)])
    perm = np.concatenate([np.arange(64), np.arange(80, 96), np.arange(64, 80)])
    sh["w_uq_b"] = np.stack([_pk(w_uq[l][:, :, perm].reshape(QL, NH * 96)) for l in range(L)])
    w_ukv = inp["w_ukv"]
    sh["w_ukv_k"] = np.stack([_pk(np.ascontiguousarray(w_ukv[l][:, :, :64]).reshape(KVL, NH * 64)) for l in range(L)])
    sh["w_ukv_v"] = np.stack([_pk(np.ascontiguousarray(w_ukv[l][:, :, 64:]).reshape(KVL, NH * 64)) for l in range(L)])
    for nm in ("w_o_attn", "w_o_fourier", "w_o_rnn", "w_out"):
        sh[nm] = np.stack([_pk(inp[nm][l]) for l in range(L)])
    sh["w_down"] = np.stack([np.ascontiguousarray(_pk(inp["w_down"][l]).reshape(128, NJ, KD, 128).transpose(2, 0, 1, 3)) for l in range(L)])
    w_up = inp["w_up"]
    wu = np.empty((L, NJ, 128, KD, 256), f)
    for l in range(L):
        a = _pk(w_up[l])
        for j in range(NJ):
            wu[l, j, :, :, :128] = a[:, :, j * 128:(j + 1) * 128]
            wu[l, j, :, :, 128:] = a[:, :, DFF + j * 128:DFF + (j + 1) * 128]
    sh["w_up"] = wu
    sh["w_ada"] = np.stack([_pk(inp["w_ada"][l]) for l in range(L)])
    sh["b_ada"] = _vec(inp["b_ada"])
    sh["norm_mix"] = _vec(inp["norm_mix"])
    sh["norm_ffn"] = _vec(inp["norm_ffn"])
    sh["q_norm"] = _vec(inp["q_norm"])
    sh["kv_norm"] = _vec(inp["kv_norm"])
    sh["final_norm"] = _vec(inp["final_norm"])
    sh["rnn_cw"] = _vec(inp["rnn_conv_w"])
    sh["rnn_cb"] = _vec(inp["rnn_conv_b"])
    sh["rg_lam"] = _vec(inp["rg_lambda"])
    sh["rg_ba"] = _vec(inp["rg_b_a"])
    sh["rg_bx"] = _vec(inp["rg_b_x"])
    rg = np.zeros((L, 128, 4, 4, 128), f)
    for l in range(L):
        for c in range(4):
            for ty, (nm, dr) in enumerate((("rg_w_a", 0), ("rg_w_x", 0), ("rg_w_a", 1), ("rg_w_x", 1))):
                for hh in range(2):
                    rg[l, hh * 64:(hh + 1) * 64, c, ty, hh * 64:(hh + 1) * 64] = inp[nm][l, dr, 2 * c + hh]
    sh["rg_w"] = rg
    fcw = inp["ffn_conv_w"]
    fcb = inp["ffn_conv_b"]
    cw = np.empty((128, L, NJ, 2, 3), f)
    cb = np.empty((128, L, NJ, 2), f)
    for l in range(L):
        for j in range(NJ):
            for g in range(2):
                sl = slice(g * DFF + j * 128, g * DFF + (j + 1) * 128)
                cw[:, l, j, g, :] = fcw[l][:, sl].T
                cb[:, l, j, g] = fcb[l][sl]
    sh["ffn_cw"] = cw
    sh["ffn_cb"] = cb
    sh.update(_tables())
    return {k: np.ascontiguousarray(v) for k, v in sh.items()}


def prep_core(inp, b):
    xc = np.concatenate([inp["ctx"][b], inp["x"][b]], axis=0)
    xT0 = np.ascontiguousarray(xc.T.reshape(KD, 128, NT).transpose(1, 0, 2))
    cv = np.stack([inp["c"][b], inp["c_ctx"]], axis=-1)
    cvec = np.ascontiguousarray(cv.reshape(KD, 128, 2).transpose(1, 0, 2))
    return {"xT0": xT0, "cvec": cvec}


def build(n_layers=L, debug=False, upto="all"):
    nc = bass.Bass("TRN2", target_bir_lowering=False)
    T = Trk(nc)

    def din(name, shape, dt=F32):
        return nc.dram_tensor(name, list(shape), dt, kind="ExternalInput").ap()

    xT0 = din("xT0", (128, KD, NT))
    cvec = din("cvec", (128, KD, 2))
    W = {}
    for name, shape in (("w_in_a", (L, 128, KD, 768)), ("w_in_f", (L, 128, KD, 512)), ("w_in_r", (L, 128, KD, 512)),
                        ("w_in_g", (L, 128, KD, 512)), ("w_in_gl", (L, 128, KD, 3072)),
                        ("w_uq_a", (L, 128, 3, 768)), ("w_uq_b", (L, 128, 3, 768)),
                        ("w_ukv_k", (L, 128, 2, 512)), ("w_ukv_v", (L, 128, 2, 512)),
                        ("w_o_attn", (L, 128, 4, D)), ("w_o_fourier", (L, 128, 4, D)), ("w_o_rnn", (L, 128, 4, D)),
                        ("w_out", (L, 128, KD, D)), ("w_down", (L, KD, 128, NJ, 128)), ("w_up", (L, NJ, 128, KD, 256)),
                        ("w_ada", (L, 128, KD, 6 * D)), ("b_ada", (128, L, 48)), ("norm_mix", (128, L, 8)),
                        ("norm_ffn", (128, L, 8)), ("q_norm", (128, L, 3)), ("kv_norm", (128, L, 2)),
                        ("final_norm", (128, 8)), ("rnn_cw", (128, L, 4, 4)), ("rnn_cb", (128, L, 4)),
                        ("rg_lam", (128, L, 2, 4)), ("rg_ba", (128, L, 2, 4)), ("rg_bx", (128, L, 2, 4)),
                        ("rg_w", (L, 128, 4, 4, 128)), ("ffn_cw", (128, L, NJ, 2, 3)), ("ffn_cb", (128, L, NJ, 2)),
                        ("rope_cos", (32, S)), ("rope_sin", (32, S))):
        W[name] = din(name, shape)
    for name, shape in (("dft_c", (128, 256)), ("dft_lc", (128, 16, S)), ("dft_ls", (128, 16, S)),
                        ("dft_cc", (128, 2, C)), ("dft_cs", (128, 2, C))):
        W[name] = din(name, shape, BF16)

    out_T = nc.dram_tensor("out_T", [128, KD, S], F32, kind="ExternalOutput").ap()

    kind_s = "ExternalOutput" if debug else "Internal"

    def dscr(name, shape, dt=BF16):
        return nc.dram_tensor(name, list(shape), dt, kind=kind_s).ap()

    hT_d = dscr("hT_d", (128, KD, NT))
    attT_d = dscr("attT_d", (128, 4, NT))
    yfT_d = dscr("yfT_d", (128, 4, NT))
    hgT_d = dscr("hgT_d", (128, 4, NT))
    mT_d = dscr("mT_d", (128, KD, NT))
    h2T_d = dscr("h2T_d", (128, KD, NT))
    dbg_x = dscr("dbg_x", (128, KD, NT), F32) if debug else None
    TB = {nm: [Buf(f"{nm}{i}") for i in range(len(TILES))] for nm in ("h", "att", "yf", "hg", "m", "h2")}

    def sb(name, shape, dt=F32):
        return nc.alloc_sbuf_tensor(name, list(shape), dt).ap()

    uid = [0]

    def sbt(name, shape, dt):
        uid[0] += 1
        return nc.sbuf_tensor(f"{name}_{uid[0]}", list(shape), dt)

    xT = sb("xT", (128, KD, NT))
    XB = [Buf(f"x{i}") for i in range(len(TILES))]
    mod = sb("mod", (128, L, 48, 2))
    modB = Buf("mod")
    cst = {}
    cstB = Buf("cst")
    for name in ("b_ada", "norm_mix", "norm_ffn", "q_norm", "kv_norm", "final_norm", "rnn_cw", "rnn_cb",
                 "rg_lam", "rg_ba", "rg_bx", "ffn_cw", "ffn_cb"):
        cst[name] = sb("c_" + name, W[name].shape)
    cl_t = sb("cl_t", (128, L, 2, 4))
    hcl_t = sb("hcl_t", (128, L, 2, 4))
    eps_t = sb("eps_t", (128, 1))
    hba_t = sb("hba_t", (128, L, 2, 4))
    hbx_t = sb("hbx_t", (128, L, 2, 4))
    ones_bf = sb("ones_bf", (128, 128), BF16)
    silu_c = sb("silu_c", (128, KD, 2), BF16)
    lay2 = sb("lay", (128, 2, 6, KD, 2))
    layB2 = [Buf("lay0"), Buf("lay1")]

    PS = [nc.alloc_psum_tensor(f"ps{i}", [128, 512], F32).ap() for i in range(8)]
    PSB = [Buf(f"ps{i}", psum=True) for i in range(8)]
    ps_rr = {"mm": [2, 3, 4, 5, 6, 7], "acc": [0, 1], "s": [2, 3, 4], "p": [5, 6, 7], "all": [0, 1, 2, 3, 4, 5, 6, 7]}
    ps_i = {k: 0 for k in ps_rr}

    def psum(pool="mm"):
        lst = ps_rr[pool]
        i = lst[ps_i[pool] % len(lst)]
        ps_i[pool] += 1
        return PS[i], PSB[i]

    CUT = int(os.environ.get("KCUT", "99"))
    for name in cst:
        T.dma_op("sp", cst[name], W[name], writes=(cstB,))
    for i, (t0, t1) in enumerate(TILES):
        T.dma_op("sp", xT[:, :, t0:t1], xT0[:, :, t0:t1], writes=(XB[i],))
    T.op("dve", lambda e: e.memset(ones_bf, 1.0), writes=(cstB,))
    T.op("dve", lambda e: e.memset(eps_t, EPS), writes=(cstB,))
    if CUT <= 0:
        n_layers = 0
    T.op("act", lambda e: e.activation(out=cl_t, in_=cst["rg_lam"], func=AF.Exp, scale=-1.0), reads=(cstB,), writes=(cstB,))
    T.op("act", lambda e: e.activation(out=cl_t, in_=cl_t, func=AF.Ln, bias=1.0), reads=(cstB,), writes=(cstB,))
    T.op("dve", lambda e: e.tensor_scalar(out=cl_t, in0=cl_t, scalar1=-8.0, scalar2=None, op0=ALU.mult), reads=(cstB,), writes=(cstB,))
    T.op("dve", lambda e: e.tensor_scalar(out=hcl_t, in0=cl_t, scalar1=0.5, scalar2=None, op0=ALU.mult), reads=(cstB,), writes=(cstB,))
    T.op("dve", lambda e: e.tensor_scalar(out=hba_t, in0=cst["rg_ba"], scalar1=0.5, scalar2=None, op0=ALU.mult), reads=(cstB,), writes=(cstB,))
    T.op("dve", lambda e: e.tensor_scalar(out=hbx_t, in0=cst["rg_bx"], scalar1=0.5, scalar2=None, op0=ALU.mult), reads=(cstB,), writes=(cstB,))
    cv_sb = sb("cv_sb", (128, KD, 2))
    T.dma_op("sp", cv_sb, cvec, writes=(cstB,))
    T.op("act", lambda e: e.activation(out=silu_c, in_=cv_sb, func=AF.Silu), reads=(cstB,), writes=(cstB,))

    if CUT <= 1:
        n_layers = 0
    with sbt("wada", [128, 2, KD, 512], BF16) as wada_h:
        wada = wada_h.ap()
        wadaB = [Buf("wada0"), Buf("wada1")]
        gi = 0
        for l in range(min(1, n_layers)):
            pm, pmB = psum("acc")
            for g in range(12):
                bi = gi % 2
                gi += 1
                T.dma_op("pool", wada[:, bi], W["w_ada"][l, :, :, g * 512:(g + 1) * 512], writes=(wadaB[bi],))
                for ff in range(4):
                    f = g * 4 + ff
                    T.mm(pmB, pm[:, 2 * f:2 * f + 2],
                         [(wada[:, bi, k, ff * 128:(ff + 1) * 128], silu_c[:, k, :]) for k in range(KD)],
                         reads=(wadaB[bi], cstB))
            T.op("dve", lambda e: e.tensor_tensor(
                out=mod[:, l], in0=pm[:, 0:96].rearrange("p (f c) -> p f c", c=2),
                in1=cst["b_ada"][:, l, :].unsqueeze(2).to_broadcast([128, 48, 2]), op=ALU.add),
                reads=(pmB, cstB), writes=(modB,))
    T.barrier()
    if CUT <= 2:
        n_layers = 0

    def load_w(dst, src, buf, q="pool"):
        T.dma_op(q, dst, src, writes=(buf,))

    def norm_modulate(l, which, dst_d, dstB, pre=None, tiles=None, rngs=None):
        i_sh, i_gs = (0, 1) if which == 0 else (3, 4)
        lay, layB = lay2[:, l % 2], layB2[l % 2]
        if rngs is None:
            tl_ = list(range(len(TILES))) if tiles is None else list(tiles)
            rngs = [TILES[ti] for ti in tl_]
        else:
            tl_ = list(range(len(rngs)))
        ovl = lambda t0, t1: [i for i, (a_, b_) in enumerate(TILES) if a_ < t1 and b_ > t0]
        with sbt("nm_sq", [128, 2, 512], BF16) as sq_h, \
                sbt("nm_rs", [128, 2, 512], F32) as rs_h, \
                sbt("nm_tmp", [128, 2, 512], F32) as tmp_h, \
                sbt("nm_h", [128, 2, KD, 512], BF16) as h_h:
            sq, rs, tmp, hh = sq_h.ap(), rs_h.ap(), tmp_h.ap(), h_h.ap()
            sqB = [Buf("sq0"), Buf("sq1")]
            rsB = [Buf("rs0"), Buf("rs1")]
            tmpB = [Buf("tmp0"), Buf("tmp1")]
            hB = [Buf("hh0"), Buf("hh1")]
            nn = [0]
            pend = {}

            def stage1(i_):
                t0, t1 = rngs[i_]
                w = t1 - t0
                xb = tuple(XB[i] for i in ovl(t0, t1))
                pm, pmB = psum("acc")
                for k in range(KD):
                    b = nn[0] % 2
                    nn[0] += 1
                    T.op("act", lambda e, k=k, b=b: e.activation(out=sq[:, b, :w], in_=xT[:, k, t0:t1], func=AF.Square),
                         reads=xb, writes=(sqB[b],))
                    for d in T._deps((sqB[b], cstB), (pmB,) if k == 0 else ()):
                        if not d[0].startswith("pe_"):
                            T._wait("pe", d)
                    ins = nc.tensor.matmul(pm[:, :w], lhsT=ones_bf, rhs=sq[:, b, :w], start=(k == 0), stop=(k == KD - 1))
                    dep = T._bump("pe")
                    ins.then_inc(dep[1], 1)
                    T._record(dep, (sqB[b], cstB), (pmB,))
                pend[i_] = (pm, pmB)

            def stage2(i_):
                t0, t1 = rngs[i_]
                w = t1 - t0
                col = 1 if t0 < C else 0
                xb = tuple(XB[i] for i in ovl(t0, t1))
                pm, pmB = pend.pop(i_)
                rb = i_ % 2
                T.op("act", lambda e: e.activation(out=rs[:, rb, :w], in_=pm[:, :w], func=AF.Ln, scale=1.0 / D, bias=eps_t[:, 0:1]),
                     reads=(pmB, cstB), writes=(rsB[rb],))
                T.op("act", lambda e: e.activation(out=rs[:, rb, :w], in_=rs[:, rb, :w], func=AF.Exp, scale=-0.5), reads=(rsB[rb],), writes=(rsB[rb],))
                for k in range(KD):
                    b = k % 2
                    T.op("dve", lambda e, k=k, b=b: e.scalar_tensor_tensor(
                        out=tmp[:, b, :w], in0=xT[:, k, t0:t1], scalar=lay[:, i_gs, k, col:col + 1], in1=rs[:, rb, :w],
                        op0=ALU.mult, op1=ALU.mult), reads=xb + (layB, rsB[rb]), writes=(tmpB[b],))
                    T.op("act", lambda e, k=k, b=b: e.activation(
                        out=hh[:, rb, k, :w], in_=tmp[:, b, :w], func=AF.Identity, bias=lay[:, i_sh, k, col:col + 1]),
                        reads=(tmpB[b], layB), writes=(hB[rb],))
                T.dma_op("sp", dst_d[:, :, t0:t1], hh[:, rb, :, :w], reads=(hB[rb],), writes=tuple(dstB[i] for i in ovl(t0, t1)))

            if pre is not None:
                pre(tl_[0])
            stage1(0)
            for i_ in range(len(rngs)):
                if i_ + 1 < len(rngs):
                    if pre is not None:
                        pre(tl_[i_ + 1])
                    stage1(i_ + 1)
                stage2(i_)

    def layer_consts(l):
        lay, layB = lay2[:, l % 2], layB2[l % 2]
        for idx, ch in ((0, 0), (2, 2), (3, 3), (5, 5)):
            T.op("dve", lambda e, idx=idx, ch=ch: e.tensor_copy(out=lay[:, idx], in_=mod[:, l, ch * 8:(ch + 1) * 8, :]),
                 reads=(modB,), writes=(layB,))
        for idx, ch, nm in ((1, 1, "norm_mix"), (4, 4, "norm_ffn")):
            T.op("dve", lambda e, idx=idx, ch=ch, nm=nm: e.scalar_tensor_tensor(
                out=lay[:, idx], in0=mod[:, l, ch * 8:(ch + 1) * 8, :], scalar=1.0,
                in1=cst[nm][:, l, :].unsqueeze(2).to_broadcast([128, KD, 2]), op0=ALU.add, op1=ALU.mult),
                reads=(modB, cstB), writes=(layB,))

    def attention(l, last):
        with sbt("a_cqn", [128, 3, NT], BF16) as cqn_h, sbt("a_ckvn", [128, 2, NT], BF16) as ckvn_h, \
                sbt("a_kr", [96, NT], BF16) as kr_h, \
                sbt("a_wuq", [128, 2, 3, 768], BF16) as wuq_h, sbt("a_wukv", [128, 2, 2, 512], BF16) as wukv_h, \
                sbt("a_rope", [96, 2, 2, 512], F32) as rope_h, sbt("a_t", [128, 4, 512], F32) as t_h, \
                sbt("a_rd", [128, 2, 512], F32) as rd_h:
            cqn, ckvn, krT = cqn_h.ap(), ckvn_h.ap(), kr_h.ap()
            wuq, wukv = wuq_h.ap(), wukv_h.ap()
            rope, tt, rd = rope_h.ap(), t_h.ap(), rd_h.ap()
            cqnB = [Buf(f"cqn{i}") for i in range(5)]
            ckvnB = [Buf(f"ckvn{i}") for i in range(5)]
            krB = [Buf(f"kr{i}") for i in range(5)]
            VB = [Buf(f"v{i}") for i in range(18)]
            wB = Buf("aw")
            kB = [[Buf(f"k{j}_{i}") for i in range(5)] for j in range(2)]
            qB = [[Buf(f"q{j}_{i}") for i in range(5)] for j in range(2)]
            EB = [Buf(f"e{i}") for i in range(4)]
            stB = [Buf(f"st{i}") for i in range(5)]
            ropeB = [Buf("rope0"), Buf("rope1")]
            tB = [Buf(f"t{i}") for i in range(4)]
            rdB = [Buf("rd0"), Buf("rd1")]
            rope_n = [0]

            def load_rope(t0, t1):
                b = rope_n[0] % 2
                rope_n[0] += 1
                w = t1 - t0
                T.dma_op("sp", rope[64:96, b, 0, :w], W["rope_cos"][:, t0 - C:t1 - C], writes=(ropeB[b],))
                T.dma_op("sp", rope[64:96, b, 1, :w], W["rope_sin"][:, t0 - C:t1 - C], writes=(ropeB[b],))
                return b

            def apply_rope(dst, dstB_, pa, paB, pb, pbB, rb, w):
                T.op("dve", lambda e: e.tensor_tensor(out=tt[64:96, 0, :w], in0=pa[64:96, :w], in1=rope[64:96, rb, 0, :w], op=ALU.mult),
                     reads=(paB, ropeB[rb]), writes=(tB[0],))
                T.op("dve", lambda e: e.tensor_tensor(out=tt[64:96, 1, :w], in0=pb[64:96, :w], in1=rope[64:96, rb, 1, :w], op=ALU.mult),
                     reads=(pbB, ropeB[rb]), writes=(tB[1],))
                T.op("dve", lambda e: e.tensor_tensor(out=dst, in0=tt[64:96, 0, :w], in1=tt[64:96, 1, :w], op=ALU.add),
                     reads=(tB[0], tB[1]), writes=(dstB_,))

            with sbt("a_win", [128, KD, 768], BF16) as win_h, sbt("a_h", [128, 2, KD, 512], BF16) as ht_h, \
                    sbt("a_raw", [128, 2, 5, 512], F32) as raw_h, sbt("a_sq", [128, 2, 5, 512], BF16) as sq_h:
                win, ht, raw, sq = win_h.ap(), ht_h.ap(), raw_h.ap(), sq_h.ap()
                winB = Buf("win")
                htB = [Buf("ht0"), Buf("ht1")]
                rawB = [[Buf(f"raw{p_}_{i}") for i in range(5)] for p_ in range(2)]
                sqB = [[Buf(f"sq{p_}_{i}") for i in range(5)] for p_ in range(2)]
                load_w(win, W["w_in_a"][l], winB)
                load_w(wukv[:, 1], W["w_ukv_v"][l], wB)
                load_w(wukv[:, 0], W["w_ukv_k"][l], wB)
                load_w(wuq[:, 0], W["w_uq_a"][l], wB)
                load_w(wuq[:, 1], W["w_uq_b"][l], wB)

                def a_load(ti):
                    t0, t1 = TILES[ti]
                    T.dma_op("sp", ht[:, ti % 2, :, :t1 - t0], hT_d[:, :, t0:t1], reads=(TB["h"][ti],), writes=(htB[ti % 2],))

                def a_stage1(ti):
                    t0, t1 = TILES[ti]
                    w = t1 - t0
                    hb = ti % 2
                    if ti + 1 < len(TILES):
                        a_load(ti + 1)
                    for c5 in range(5):
                        pm, pmB = psum("mm")
                        T.mm(pmB, pm[:, :w], [(win[:, k, c5 * 128:(c5 + 1) * 128], ht[:, hb, k, :w]) for k in range(KD)],
                             reads=(winB, htB[hb]))
                        T.op("act", lambda e, c5=c5, pm=pm: e.activation(out=sq[:, hb, c5, :w], in_=pm[:, :w], func=AF.Square),
                             reads=(pmB,), writes=(sqB[hb][c5],))
                        T.op("dve", lambda e, c5=c5, pm=pm: e.tensor_copy(out=raw[:, hb, c5, :w], in_=pm[:, :w]),
                             reads=(pmB,), writes=(rawB[hb][c5],))
                    pa, paB = psum("mm")
                    T.mm(paB, pa[0:96, :w], [(win[:, k, 576:672], ht[:, hb, k, :w]) for k in range(KD)], reads=(winB, htB[hb]))
                    if ti == 0:
                        T.op("act", lambda e, pa=pa: e.activation(out=krT[64:96, t0:t1], in_=pa[64:96, :w], func=AF.Copy),
                             reads=(paB,), writes=(krB[ti],))
                    else:
                        pb, pbB = psum("mm")
                        T.mm(pbB, pb[0:96, :w], [(win[:, k, 672:768], ht[:, hb, k, :w]) for k in range(KD)], reads=(winB, htB[hb]))
                        rb = load_rope(t0, t1)
                        apply_rope(krT[64:96, t0:t1], krB[ti], pa, paB, pb, pbB, rb, w)

                def a_stage2(ti):
                    t0, t1 = TILES[ti]
                    w = t1 - t0
                    hb = ti % 2
                    for (cs, dstn, dstBn, gname, nfeat) in ((range(0, 3), cqn, cqnB, "q_norm", QL), (range(3, 5), ckvn, ckvnB, "kv_norm", KVL)):
                        pm, pmB = psum("mm")
                        T.mm(pmB, pm[:, :w], [(ones_bf, sq[:, hb, c5, :w]) for c5 in cs], reads=tuple(sqB[hb][c5] for c5 in cs) + (cstB,))
                        rb = 0 if nfeat == QL else 1
                        T.op("act", lambda e, pm=pm, rb=rb, nfeat=nfeat: e.activation(out=rd[:, rb, :w], in_=pm[:, :w], func=AF.Ln, scale=1.0 / nfeat, bias=eps_t[:, 0:1]),
                             reads=(pmB, cstB), writes=(rdB[rb],))
                        T.op("act", lambda e, rb=rb: e.activation(out=rd[:, rb, :w], in_=rd[:, rb, :w], func=AF.Exp, scale=-0.5), reads=(rdB[rb],), writes=(rdB[rb],))
                        for ci, c5 in enumerate(cs):
                            T.op("dve", lambda e, ci=ci, c5=c5, rb=rb, dstn=dstn, gname=gname: e.scalar_tensor_tensor(
                                out=dstn[:, ci, t0:t1], in0=raw[:, hb, c5, :w], scalar=cst[gname][:, l, ci:ci + 1], in1=rd[:, rb, :w],
                                op0=ALU.mult, op1=ALU.mult), reads=(rawB[hb][c5], cstB, rdB[rb]), writes=(dstBn[ti],))

                a_load(0)
                a_stage1(0)
                for ti in range(len(TILES)):
                    if ti + 1 < len(TILES):
                        a_stage1(ti + 1)
                    a_stage2(ti)
            T.barrier()
            with sbt("a_v", [128, 18, 4, 192], BF16) as v_h, sbt("a_k", [96, 2, NT], BF16) as k_h, \
                    sbt("a_q", [96, 2, NT], BF16) as q_h, sbt("a_e", [128, 4, 512], BF16) as e_h, \
                    sbt("a_st", [128, NT], BF16) as st_h:
                Va, kT, qT, Eb, st = v_h.ap(), k_h.ap(), q_h.ap(), e_h.ap(), st_h.ap()
                T.op("pool", lambda e: e.memset(Va[:, :, :, 64:128], 1.0), writes=tuple(VB))
                for kt in range(18):
                    ti = 0 if kt < 2 else 1 + (kt - 2) // 4
                    pm, pmB = psum("mm")
                    T.mm(pmB, pm[:, :], [(ckvn[:, c2, kt * 128:(kt + 1) * 128], wukv[:, 1, c2, :]) for c2 in range(2)],
                         reads=(ckvnB[ti], wB))
                    pv = pm.rearrange("p (a h e) -> p a h e", a=4, h=2)
                    T.op("act", lambda e, pv=pv, kt=kt: e.activation(out=Va[:, kt, :, 0:64], in_=pv[:, :, 0, :], func=AF.Copy),
                         reads=(pmB,), writes=(VB[kt],))
                    T.op("dve", lambda e, pv=pv, kt=kt: e.tensor_copy(out=Va[:, kt, :, 128:192], in_=pv[:, :, 1, :]),
                         reads=(pmB,), writes=(VB[kt],))
                sc = 1.0 / math.sqrt(96.0)
                en = [0]
                LA = 2

                def proj_head(h):
                    hb2 = h % 2
                    for ti, (t0, t1) in enumerate(TILES):
                        w = t1 - t0
                        pm, pmB = psum("p")
                        T.mm(pmB, pm[0:64, :w], [(wukv[:, 0, c2, h * 64:(h + 1) * 64], ckvn[:, c2, t0:t1]) for c2 in range(2)],
                             reads=(wB, ckvnB[ti]))
                        T.op("dve", lambda e, pm=pm: e.tensor_copy(out=kT[0:64, hb2, t0:t1], in_=pm[0:64, :w]), reads=(pmB,), writes=(kB[hb2][ti],))
                        T.op("pool", lambda e: e.tensor_copy(out=kT[64:96, hb2, t0:t1], in_=krT[64:96, t0:t1]), reads=(krB[ti],), writes=(kB[hb2][ti],))
                        if ti == 0 and last:
                            continue
                        pa, paB = psum("p")
                        T.mm(paB, pa[0:96, :w], [(wuq[:, 0, c3, h * 96:(h + 1) * 96], cqn[:, c3, t0:t1]) for c3 in range(3)],
                             reads=(wB, cqnB[ti]))
                        T.op("dve", lambda e, pa=pa: e.tensor_copy(out=qT[0:64, hb2, t0:t1], in_=pa[0:64, :w]), reads=(paB,), writes=(qB[hb2][ti],))
                        if ti == 0:
                            T.op("dve", lambda e, pa=pa: e.tensor_copy(out=qT[64:96, hb2, t0:t1], in_=pa[64:96, :w]), reads=(paB,), writes=(qB[hb2][ti],))
                        else:
                            pb, pbB = psum("p")
                            T.mm(pbB, pb[0:96, :w], [(wuq[:, 1, c3, h * 96:(h + 1) * 96], cqn[:, c3, t0:t1]) for c3 in range(3)],
                                 reads=(wB, cqnB[ti]))
                            rb = load_rope(t0, t1)
                            apply_rope(qT[64:96, hb2, t0:t1], qB[hb2][ti], pa, paB, pb, pbB, rb, w)

                proj_head(0)
                for h in range(NH):
                    pr, od = h // 2, h % 2
                    hb2 = h % 2
                    vsl = slice(0, 128) if od == 0 else slice(64, 192)
                    nsl, dsl = (slice(0, 64), slice(64, 128)) if od == 0 else (slice(64, 128), slice(0, 64))
                    items = []
                    for ti, (t0, t1) in enumerate(TILES):
                        if ti == 0 and last:
                            continue
                        kts = list(range(2)) if ti == 0 else list(range(18))
                        for kt in kts:
                            items.append((ti, kt, kt == kts[0], kt == kts[-1]))
                    sq_ = {}

                    def issue_s(i):
                        ti, kt, _, _ = items[i]
                        t0, t1 = TILES[ti]
                        kti = 0 if kt < 2 else 1 + (kt - 2) // 4
                        pm, pmB = psum("s")
                        T.mm(pmB, pm[:, :t1 - t0], [(kT[0:96, hb2, kt * 128:(kt + 1) * 128], qT[0:96, hb2, t0:t1])], reads=(kB[hb2][kti], qB[hb2][ti]))
                        sq_[i] = (pm, pmB)

                    for i in range(min(LA, len(items))):
                        issue_s(i)
                    po = poB = None
                    mid = len(items) // 2
                    for i, (ti, kt, first, lastk) in enumerate(items):
                        t0, t1 = TILES[ti]
                        w = t1 - t0
                        if i + LA < len(items):
                            issue_s(i + LA)
                        if i == mid and h + 1 < NH:
                            proj_head(h + 1)
                        if first:
                            po, poB = psum("acc")
                        pm, pmB = sq_.pop(i)
                        eb = en[0] % 4
                        en[0] += 1
                        T.op("act", lambda e, pm=pm, eb=eb, w=w: e.activation(out=Eb[:, eb, :w], in_=pm[:, :w], func=AF.Exp, scale=sc),
                             reads=(pmB,), writes=(EB[eb],))
                        for d in T._deps((EB[eb], VB[kt]), (poB,) if first else ()):
                            if not d[0].startswith("pe_"):
                                T._wait("pe", d)
                        ins = nc.tensor.matmul(po[:, :w], lhsT=Va[:, kt, pr, vsl], rhs=Eb[:, eb, :w], start=first, stop=lastk)
                        dep = T._bump("pe")
                        ins.then_inc(dep[1], 1)
                        T._record(dep, (EB[eb], VB[kt]), (poB,))
                        if lastk:
                            rb = od
                            T.op("dve", lambda e, po=po, rb=rb, w=w: e.reciprocal(out=rd[dsl, rb, :w], in_=po[dsl, :w]), reads=(poB,), writes=(rdB[rb],))
                            T.op("dve", lambda e, po=po, rb=rb, w=w, t0=t0, t1=t1: e.tensor_tensor(out=st[nsl, t0:t1], in0=po[nsl, :w], in1=rd[dsl, rb, :w], op=ALU.mult),
                                 reads=(poB, rdB[rb]), writes=(stB[ti],))
                    if od == 1:
                        for ti, (t0, t1) in enumerate(TILES):
                            if ti == 0 and last:
                                continue
                            T.dma_op("sp", attT_d[:, pr, t0:t1], st[:, t0:t1], reads=(stB[ti],), writes=(TB["att"][ti],))
        T.barrier()

    def fourier(l, last):
        with sbt("f_ab", [128, 18, 4, 256], BF16) as ab_h:
            AB = ab_h.ap()
            ABB = [Buf(f"ab{i}") for i in range(18)]
            with sbt("f_w", [128, KD, 512], BF16) as wf_h, sbt("f_h", [128, 2, KD, 512], BF16) as ht_h, \
                    sbt("f_u", [128, 2, 4, 512], BF16) as uf_h, sbt("f_c", [128, 256], BF16) as csc_h:
                wf, ht, uf, csc = wf_h.ap(), ht_h.ap(), uf_h.ap(), csc_h.ap()
                wfB, cscB = Buf("wf"), Buf("csc")
                htB = [Buf("fht0"), Buf("fht1")]
                ufB = [Buf("uf0"), Buf("uf1")]
                load_w(wf, W["w_in_f"][l], wfB)
                T.dma_op("sp", csc, W["dft_c"], writes=(cscB,))
                n = 0
                fseq = [ti for ti in range(len(TILES)) if not (ti == 0 and last)]

                def f_load(ti):
                    t0, t1 = TILES[ti]
                    T.dma_op("sp", ht[:, ti % 2, :, :t1 - t0], hT_d[:, :, t0:t1], reads=(TB["h"][ti],), writes=(htB[ti % 2],))

                f_load(fseq[0])
                for fi, ti in enumerate(fseq):
                    t0, t1 = TILES[ti]
                    w = t1 - t0
                    hb = ti % 2
                    if fi + 1 < len(fseq):
                        f_load(fseq[fi + 1])
                    for g in range(4):
                        pm, pmB = psum("mm")
                        T.mm(pmB, pm[:, :w], [(wf[:, k, g * 128:(g + 1) * 128], ht[:, hb, k, :w]) for k in range(KD)], reads=(wfB, htB[hb]))
                        if g % 2 == 0:
                            T.op("act", lambda e, pm=pm, g=g: e.activation(out=uf[:, hb, g, :w], in_=pm[:, :w], func=AF.Copy), reads=(pmB,), writes=(ufB[hb],))
                        else:
                            T.op("dve", lambda e, pm=pm, g=g: e.tensor_copy(out=uf[:, hb, g, :w], in_=pm[:, :w]), reads=(pmB,), writes=(ufB[hb],))
                    for sub in range(w // 128):
                        kt = t0 // 128 + sub
                        for gp in range(2):
                            pm, pmB = psum("mm")
                            for g2 in range(2):
                                g = gp * 2 + g2
                                T.mm(pmB, pm[:, g2 * 256:(g2 + 1) * 256], [(uf[:, hb, g, sub * 128:(sub + 1) * 128], csc)], reads=(ufB[hb], cscB))
                            dst = AB[:, kt, gp * 2:(gp + 1) * 2, :].rearrange("p a b -> p (a b)")
                            if n % 2 == 0:
                                T.op("act", lambda e, pm=pm, dst=dst: e.activation(out=dst, in_=pm, func=AF.Copy), reads=(pmB,), writes=(ABB[kt],))
                            else:
                                T.op("dve", lambda e, pm=pm, dst=dst: e.tensor_copy(out=dst, in_=pm), reads=(pmB,), writes=(ABB[kt],))
                            n += 1
            T.barrier()
            with sbt("f_tl", [128, 2, 2, 16, 512], BF16) as tl_h, sbt("f_tc", [128, 2, 2, 256], BF16) as tc_h, \
                    sbt("f_y", [128, 2, 4, 512], BF16) as y_h:
                tl, tcx, yst = tl_h.ap(), tc_h.ap(), y_h.ap()
                tlB = [Buf("tl0"), Buf("tl1")]
                tcB = Buf("tcx")
                yB = [Buf("y0"), Buf("y1")]
                n = 0
                if not last:
                    T.dma_op("sp", tcx[:, 0], W["dft_cc"], writes=(tcB,))
                    T.dma_op("sp", tcx[:, 1], W["dft_cs"], writes=(tcB,))
                    yb = n % 2
                    n += 1
                    for g in range(4):
                        pm, pmB = psum("mm")
                        T.mm(pmB, pm[:, :C], [(AB[:, j, g, 0:128], tcx[:, 0, j, :]) for j in range(2)] +
                             [(AB[:, j, g, 128:256], tcx[:, 1, j, :]) for j in range(2)], reads=(ABB[0], ABB[1], tcB))
                        T.op("act", lambda e, pm=pm, g=g: e.activation(out=yst[:, yb, g, :C], in_=pm[:, :C], func=AF.Copy, scale=1.0 / math.sqrt(C * 128.0)),
                             reads=(pmB,), writes=(yB[yb],))
                    T.dma_op("sp", yfT_d[:, :, 0:C], yst[:, yb, :, :C], reads=(yB[yb],), writes=(TB["yf"][0],))
                def t_load(ot):
                    for hf_ in range(2):
                        js = slice(hf_ * 8, (hf_ + 1) * 8)
                        T.dma_op("sp", tl[:, ot % 2, 0, js], W["dft_lc"][:, js, ot * 512:(ot + 1) * 512], writes=(tlB[ot % 2],))
                        T.dma_op("sp", tl[:, ot % 2, 1, js], W["dft_ls"][:, js, ot * 512:(ot + 1) * 512], writes=(tlB[ot % 2],))

                t_load(0)
                for ot in range(4):
                    tb = ot % 2
                    if ot + 1 < 4:
                        t_load(ot + 1)
                    yb = n % 2
                    n += 1
                    for g in range(4):
                        pm, pmB = psum("mm")
                        T.mm(pmB, pm, [(AB[:, 2 + j, g, 0:128], tl[:, tb, 0, j, :]) for j in range(16)] +
                             [(AB[:, 2 + j, g, 128:256], tl[:, tb, 1, j, :]) for j in range(16)], reads=tuple(ABB[2:]) + (tlB[tb],))
                        if g % 2 == 0:
                            T.op("act", lambda e, pm=pm, g=g: e.activation(out=yst[:, yb, g, :], in_=pm, func=AF.Copy, scale=1.0 / 512.0), reads=(pmB,), writes=(yB[yb],))
                        else:
                            T.op("dve", lambda e, pm=pm, g=g: e.tensor_scalar(out=yst[:, yb, g, :], in0=pm, scalar1=1.0 / 512.0, scalar2=None, op0=ALU.mult), reads=(pmB,), writes=(yB[yb],))
                    T.dma_op("sp", yfT_d[:, :, C + ot * 512:C + (ot + 1) * 512], yst[:, yb], reads=(yB[yb],), writes=(TB["yf"][1 + ot],))
        T.barrier()

    def rnn(l, last):
        RT = [(0, C, 0, C)] + [(C + i * 410, min(C + (i + 1) * 410, NT), C, NT) for i in range(5)]
        with sbt("r_w", [128, 2, KD, 512], BF16) as w_h, sbt("r_rgw", [128, 4, 4, 128], BF16) as rgw_h, \
                sbt("r_h", [128, 2, KD, 416], BF16) as ht_h, sbt("r_gug", [128, 2, NT], BF16) as gug_h, \
                sbt("r_uc", [128, 2, NT], F32) as uc_h, sbt("r_ucb", [128, 2, NT], BF16) as ucb_h, \
                sbt("r_G", [128, 2, NT], F32) as G_h, sbt("r_sc", [128, NT], F32) as sc_h, sbt("r_hf", [128, NT], F32) as hf_h, \
                sbt("r_hb", [128, NT], F32) as hb_h, sbt("r_gb1", [128, NT], F32) as hg_h:
            wrg, rgw, ht, gug, uc, ucb = w_h.ap(), rgw_h.ap(), ht_h.ap(), gug_h.ap(), uc_h.ap(), ucb_h.ap()
            G, sc_, hf, hbk, gb1 = G_h.ap(), sc_h.ap(), hf_h.ap(), hb_h.ap(), hg_h.ap()
            A1B = [Buf(f"A1_{i}") for i in range(5)]
            B1B = [Buf(f"B1_{i}") for i in range(5)]
            wB = [Buf("rw0"), Buf("rw1")]
            rgwB = Buf("rgw")
            htB = [Buf("rht0"), Buf("rht1")]
            gugB = [[Buf(f"gug{a_}_{i}") for i in range(6)] for a_ in range(2)]
            ucT = [[Buf(f"uc{a_}_{i}") for i in range(6)] for a_ in range(2)]
            ucbT = [[Buf(f"ucb{a_}_{i}") for i in range(6)] for a_ in range(2)]
            G0B = [Buf(f"G0_{i}") for i in range(5)]
            G1B = [Buf(f"G1_{i}") for i in range(5)]
            scB = [Buf(f"sc_{i}") for i in range(5)]
            hfB = Buf("hf")
            seqA = [(c, ri) for c in range(4) for ri in range(6)]

            def rng(ri):
                o0, o1, lo, hi = RT[ri]
                return o0, o1, lo, hi, max(o0 - 2, lo), min(o1 + 1, hi)

            def a_hload(n):
                c, ri = seqA[n]
                o0, o1, lo, hi, c0, c1 = rng(ri)
                tis = [i for i, (a_, b_) in enumerate(TILES) if a_ < c1 and b_ > c0]
                T.dma_op("sp", ht[:, n % 2, :, :c1 - c0], hT_d[:, :, c0:c1], reads=tuple(TB["h"][i] for i in tis), writes=(htB[n % 2],))

            def a_wload(c):
                if c == 0:
                    load_w(wrg[:, 0], W["w_in_r"][l], wB[0])
                    load_w(wrg[:, 1], W["w_in_g"][l], wB[0])

            def stageA(c):
                cb = c % 2
                cw = lambda k: cst["rnn_cw"][:, l, k, c:c + 1]
                for ri in range(6):
                    n = c * 6 + ri
                    if n + 1 < len(seqA):
                        a_hload(n + 1)
                    hb_ = n % 2
                    o0, o1, lo, hi, c0, c1 = rng(ri)
                    ncp, no, off = c1 - c0, o1 - o0, o0 - c0
                    pu, puB = psum("mm")
                    T.mm(puB, pu[:, :ncp], [(wrg[:, 0, k, c * 128:(c + 1) * 128], ht[:, hb_, k, :ncp]) for k in range(KD)], reads=(wB[0], htB[hb_]))
                    pg, pgB = psum("mm")
                    T.mm(pgB, pg[:, :ncp], [(wrg[:, 1, k, c * 128:(c + 1) * 128], ht[:, hb_, k, :ncp]) for k in range(KD)], reads=(wB[0], htB[hb_]))
                    T.op("act", lambda e, pu=pu: e.activation(out=uc[:, cb, o0:o1], in_=pu[:, off:off + no], func=AF.Identity,
                                                           scale=cw(2), bias=cst["rnn_cb"][:, l, c:c + 1]), reads=(puB, cstB), writes=(ucT[cb][ri],))
                    for k, sh in ((0, -2), (1, -1), (3, 1)):
                        ta = max(o0, lo - sh) if sh < 0 else o0
                        tb = o1 if sh < 0 else min(o1, hi - sh)
                        T.op("dve", lambda e, pu=pu, k=k, sh=sh, ta=ta, tb=tb: e.scalar_tensor_tensor(
                            out=uc[:, cb, ta:tb], in0=pu[:, ta + sh - c0:tb + sh - c0], scalar=cw(k), in1=uc[:, cb, ta:tb],
                            op0=ALU.mult, op1=ALU.add), reads=(puB, cstB, ucT[cb][ri]), writes=(ucT[cb][ri],))
                    T.op("act", lambda e, pg=pg: e.activation(out=gug[:, cb, o0:o1], in_=pg[:, off:off + no], func=AF.Gelu_apprx_tanh),
                         reads=(pgB,), writes=(gugB[cb][ri],))
                    T.op("pool", lambda e: e.tensor_copy(out=ucb[:, cb, o0:o1], in_=uc[:, cb, o0:o1]), reads=(ucT[cb][ri],), writes=(ucbT[cb][ri],))

            def stageB(c):
                cb = c % 2
                for d in range(2):
                    if d == 0:
                        gA = lambda t0, t1: G[:, 0, t0:t1]
                        gB = lambda t0, t1: G[:, 1, t0:t1]
                        AB_, BB_ = G0B, G1B
                    else:
                        gA = lambda t0, t1: hbk[:, t0:t1]
                        gB = lambda t0, t1: gb1[:, t0:t1]
                        AB_, BB_ = A1B, B1B
                    for ti, (t0, t1) in enumerate(TILES):
                        w = t1 - t0
                        for gi_, (gX, XB_, hbt) in enumerate(((gA, AB_, hba_t), (gB, BB_, hbx_t))):
                            pm, pmB = psum("mm")
                            T.mm(pmB, pm[:, :w], [(rgw[:, c, 2 * d + gi_, :], ucb[:, cb, t0:t1])], reads=(rgwB,) + tuple(ucbT[cb]))
                            T.op("act", lambda e, pm=pm, gX=gX, hbt=hbt: e.activation(out=gX(t0, t1), in_=pm[:, :w], func=AF.Tanh, scale=0.5,
                                                                                   bias=hbt[:, l, d, c:c + 1]), reads=(pmB, cstB), writes=(XB_[ti],))
                        T.op("act", lambda e: e.activation(out=gA(t0, t1), in_=gA(t0, t1), func=AF.Exp, scale=hcl_t[:, l, d, c:c + 1],
                                                           bias=hcl_t[:, l, d, c:c + 1]), reads=(AB_[ti], cstB), writes=(AB_[ti],))
                        T.op("pool", lambda e: e.tensor_tensor(out=sc_[:, t0:t1], in0=gA(t0, t1), in1=gA(t0, t1), op=ALU.mult),
                             reads=(AB_[ti],), writes=(scB[ti],))
                        T.op("pool", lambda e: e.tensor_scalar(out=sc_[:, t0:t1], in0=sc_[:, t0:t1], scalar1=1.0, scalar2=0.0, op0=ALU.min, op1=ALU.max),
                             reads=(scB[ti],), writes=(scB[ti],))
                        T.op("dve", lambda e: e.scalar_tensor_tensor(out=gB(t0, t1), in0=gB(t0, t1), scalar=1.0, in1=uc[:, cb, t0:t1],
                                                                     op0=ALU.add, op1=ALU.mult), reads=(BB_[ti],) + tuple(ucT[cb]), writes=(BB_[ti],))
                    for ti, (t0, t1) in enumerate(TILES):
                        T.op("act", lambda e: e.activation(out=sc_[:, t0:t1], in_=sc_[:, t0:t1], func=AF.Sqrt, scale=-1.0, bias=1.0),
                             reads=(scB[ti],), writes=(scB[ti],))
                        T.op("dve", lambda e: e.scalar_tensor_tensor(out=gB(t0, t1), in0=gB(t0, t1), scalar=0.5, in1=sc_[:, t0:t1],
                                                                     op0=ALU.mult, op1=ALU.mult), reads=(BB_[ti], scB[ti]), writes=(BB_[ti],))
                    if d == 0:
                        T.op("dve", lambda e: e.tensor_tensor_scan(out=hf, data0=G[:, 0, :], data1=G[:, 1, :], initial=0.0, op0=ALU.mult, op1=ALU.add),
                             reads=tuple(G0B) + tuple(G1B), writes=(hfB,))
                    else:
                        T.op("dve", lambda e: e.tensor_tensor_scan(out=G[:, 0, 0:C][:, ::-1], data0=hbk[:, 0:C][:, ::-1], data1=gb1[:, 0:C][:, ::-1],
                                                                   initial=0.0, op0=ALU.mult, op1=ALU.add), reads=(A1B[0], B1B[0]), writes=tuple(G0B))
                        T.op("dve", lambda e: e.tensor_tensor_scan(out=G[:, 0, C:NT][:, ::-1], data0=hbk[:, C:NT][:, ::-1], data1=gb1[:, C:NT][:, ::-1],
                                                                   initial=G[:, 0, 0:1], op0=ALU.mult, op1=ALU.add),
                             reads=tuple(A1B) + tuple(B1B) + tuple(G0B), writes=tuple(G0B))
                T.op("dve", lambda e: e.tensor_tensor(out=hf, in0=hf, in1=G[:, 0, :], op=ALU.add), reads=(hfB,) + tuple(G0B), writes=(hfB,))
                T.op("dve", lambda e: e.tensor_tensor(out=ucb[:, cb, :], in0=hf, in1=gug[:, cb, :], op=ALU.mult),
                     reads=(hfB,) + tuple(gugB[cb]), writes=tuple(ucbT[cb]))
                T.dma_op("sp", hgT_d[:, c, :], ucb[:, cb, :], reads=tuple(ucbT[cb]), writes=tuple(TB["hg"]))

            a_hload(0)
            a_wload(0)
            a_wload(1)
            load_w(rgw, W["rg_w"][l], rgwB)
            stageA(0)
            for c in range(4):
                if c + 1 < 4:
                    if c + 2 < 4:
                        pass
                    stageA(c + 1)
                    if c + 2 < 4:
                        a_wload(c + 2)
                stageB(c)
        T.barrier()

    def merge(l, last):
        with sbt("m_wo", [128, 2, 3, 4, 256], BF16) as wo_h, sbt("m_wgl", [128, 2, KD, 3, 256], BF16) as wgl_h, \
                sbt("m_h", [128, 3, KD, 512], BF16) as ht_h, sbt("m_br", [128, 3, 3, 4, 512], BF16) as br_h, \
                sbt("m_g", [128, 2, 512], F32) as g_h, sbt("m_acc", [128, 2, 512], F32) as acc_h, \
                sbt("m_tmp", [128, 2, 512], F32) as tmp_h, sbt("m_st", [128, 2, 2, 512], BF16) as st_h:
            wo, wgl, ht, brt, gs, acc, tmp, mst = wo_h.ap(), wgl_h.ap(), ht_h.ap(), br_h.ap(), g_h.ap(), acc_h.ap(), tmp_h.ap(), st_h.ap()
            wB = [Buf("mw0"), Buf("mw1")]
            htB = [Buf("mht0"), Buf("mht1"), Buf("mht2")]
            brB = [Buf("mbr0"), Buf("mbr1"), Buf("mbr2")]
            gB = [Buf("mg0"), Buf("mg1")]
            accB = [Buf("macc0"), Buf("macc1")]
            tmpB = [Buf("mtmp0"), Buf("mtmp1")]
            stB = [Buf("mst0"), Buf("mst1")]
            srcs = ((attT_d, "att"), (yfT_d, "yf"), (hgT_d, "hg"))
            wnames = ("w_o_attn", "w_o_fourier", "w_o_rnn")
            gn = 0
            NQ = 4
            seq = [(qd, ti) for qd in range(NQ) for ti in range(len(TILES)) if not (ti == 0 and last)]

            def m_load(n):
                qd, ti = seq[n]
                t0, t1 = TILES[ti]
                w = t1 - t0
                b = n % 3
                T.dma_op("sp", ht[:, b, :, :w], hT_d[:, :, t0:t1], reads=(TB["h"][ti],), writes=(htB[b],))
                for br, (src, nm) in enumerate(srcs):
                    T.dma_op("sp", brt[:, b, br, :, :w], src[:, :, t0:t1], reads=(TB[nm][ti],), writes=(brB[b],))

            def m_wload(qd):
                qs = slice(qd * 256, (qd + 1) * 256)
                for br in range(3):
                    load_w(wo[:, qd % 2, br], W[wnames[br]][l][:, :, qs], wB[qd % 2])
                    load_w(wgl[:, qd % 2, :, br, :], W["w_in_gl"][l][:, :, br * 1024 + qd * 256:br * 1024 + (qd + 1) * 256], wB[qd % 2])

            m_load(0)
            m_wload(0)
            if len(seq) > 1:
                m_load(1)
            for n, (qd, ti) in enumerate(seq):
                t0, t1 = TILES[ti]
                w = t1 - t0
                b = n % 3
                sbi = n % 2
                wq = qd % 2
                if (n == 0 or seq[n - 1][0] != qd) and qd + 1 < NQ:
                    m_wload(qd + 1)
                if n + 2 < len(seq):
                    m_load(n + 2)
                for oc2 in range(2):
                    ab = oc2 % 2
                    for br in range(3):
                        py, pyB = psum("mm")
                        T.mm(pyB, py[:, :w], [(wo[:, wq, br, k, oc2 * 128:(oc2 + 1) * 128], brt[:, b, br, k, :w]) for k in range(4)], reads=(wB[wq], brB[b]))
                        pg, pgB = psum("mm")
                        T.mm(pgB, pg[:, :w], [(wgl[:, wq, k, br, oc2 * 128:(oc2 + 1) * 128], ht[:, b, k, :w]) for k in range(KD)], reads=(wB[wq], htB[b]))
                        gb = gn % 2
                        gn += 1
                        T.op("act", lambda e, pg=pg, gb=gb: e.activation(out=gs[:, gb, :w], in_=pg[:, :w], func=AF.Sigmoid), reads=(pgB,), writes=(gB[gb],))
                        if br == 0:
                            T.op("dve", lambda e, py=py, gb=gb, ab=ab: e.tensor_tensor(out=acc[:, ab, :w], in0=py[:, :w], in1=gs[:, gb, :w], op=ALU.mult),
                                 reads=(pyB, gB[gb]), writes=(accB[ab],))
                        else:
                            T.op("dve", lambda e, py=py, gb=gb: e.tensor_tensor(out=tmp[:, gb, :w], in0=py[:, :w], in1=gs[:, gb, :w], op=ALU.mult),
                                 reads=(pyB, gB[gb]), writes=(tmpB[gb],))
                            if br == 1:
                                T.op("dve", lambda e, gb=gb, ab=ab: e.tensor_tensor(out=acc[:, ab, :w], in0=acc[:, ab, :w], in1=tmp[:, gb, :w], op=ALU.add),
                                     reads=(accB[ab], tmpB[gb]), writes=(accB[ab],))
                            else:
                                T.op("dve", lambda e, gb=gb, ab=ab, oc2=oc2: e.tensor_tensor(out=mst[:, sbi, oc2, :w], in0=acc[:, ab, :w], in1=tmp[:, gb, :w], op=ALU.add),
                                     reads=(accB[ab], tmpB[gb]), writes=(stB[sbi],))
                T.dma_op("sp", mT_d[:, qd * 2:(qd + 1) * 2, t0:t1], mst[:, sbi, :, :w], reads=(stB[sbi],), writes=(TB["m"][ti],))
        T.barrier()

    def merge2_and_norm(l, last):
        with sbt("m_wout", [128, KD, D], BF16) as wout_h, sbt("m_m", [128, 2, KD, 512], BF16) as mt_h:
            wout, mt = wout_h.ap(), mt_h.ap()
            woB = Buf("wout")
            mtB = [Buf("mt0"), Buf("mt1")]
            load_w(wout, W["w_out"][l], woB)
            seq2 = [ti for ti in range(len(TILES)) if not (ti == 0 and last)]

            def m2_load(n):
                ti = seq2[n]
                t0, t1 = TILES[ti]
                T.dma_op("sp", mt[:, n % 2, :, :t1 - t0], mT_d[:, :, t0:t1], reads=(TB["m"][ti],), writes=(mtB[n % 2],))

            m2_load(0)

            def pre(ti):
                n = seq2.index(ti)
                t0, t1 = TILES[ti]
                w = t1 - t0
                col = 1 if ti == 0 else 0
                b = n % 2
                if n + 1 < len(seq2):
                    m2_load(n + 1)
                for oc in range(KD):
                    pm, pmB = psum("mm")
                    T.mm(pmB, pm[:, :w], [(wout[:, k, oc * 128:(oc + 1) * 128], mt[:, b, k, :w]) for k in range(KD)], reads=(woB, mtB[b]))
                    T.op("dve", lambda e, pm=pm, oc=oc: e.scalar_tensor_tensor(out=xT[:, oc, t0:t1], in0=pm[:, :w], scalar=lay2[:, l % 2, 2, oc, col:col + 1],
                                                                           in1=xT[:, oc, t0:t1], op0=ALU.mult, op1=ALU.add),
                         reads=(pmB, layB2[l % 2], XB[ti]), writes=(XB[ti],))

            norm_modulate(l, 1, h2T_d, TB["h2"], pre=pre, tiles=seq2)
        T.barrier()

    def ffn(l, last):
        units = []
        if last:
            units.append([(256, 766, 256, NT)])
        else:
            units.append([(0, 256, 0, 256), (256, 766, 256, NT)])
        units.append([(766, 1534, 256, NT)])
        units.append([(1534, 2304, 256, NT)])
        AW = 772
        with sbt("f_a", [128, NJ, 770], BF16) as a_h, sbt("f_h2", [128, KD, AW], BF16) as h2_h, \
                sbt("f_wup", [128, 3, KD, 256], BF16) as wup_h, \
                sbt("f_cu", [128, 2, 2, 770], F32) as cu_h, sbt("f_wd", [128, 2, NJ, 128], BF16) as wd_h, \
                sbt("f_wada", [128, 2, KD, 512], BF16) as wada_h:
            a_sb, h2u, wup, cu, wd, wada = a_h.ap(), h2_h.ap(), wup_h.ap(), cu_h.ap(), wd_h.ap(), wada_h.ap()
            wadaB = [Buf("fwada0"), Buf("fwada1")]
            ada_t = [0]

            def ada_tick():
                if l + 1 >= n_layers:
                    return
                t = ada_t[0]
                ada_t[0] += 1
                if t < 12:
                    T.dma_op("pool", wada[:, t % 2], W["w_ada"][l + 1, :, :, t * 512:(t + 1) * 512], writes=(wadaB[t % 2],))
                g = t - 1
                if 0 <= g < 12:
                    pm, pmB = psum("all")
                    for ff in range(4):
                        T.mm(pmB, pm[:, 2 * ff:2 * ff + 2],
                             [(wada[:, g % 2, k, ff * 128:(ff + 1) * 128], silu_c[:, k, :]) for k in range(KD)],
                             reads=(wadaB[g % 2], cstB))
                    T.op("dve", lambda e, pm=pm, g=g: e.tensor_tensor(
                        out=mod[:, l + 1, g * 4:(g + 1) * 4, :], in0=pm[:, 0:8].rearrange("p (f c) -> p f c", c=2),
                        in1=cst["b_ada"][:, l + 1, g * 4:(g + 1) * 4].unsqueeze(2).to_broadcast([128, 4, 2]), op=ALU.add),
                        reads=(pmB, cstB), writes=(modB,))

            aB = [Buf(f"a{j}") for j in range(NJ)]
            h2B = Buf("h2u")
            wupB = [Buf("wup0"), Buf("wup1"), Buf("wup2")]
            cuB = [[[Buf(f"cu{a_}_{g_}_{o_}") for o_ in range(3)] for g_ in range(2)] for a_ in range(2)]
            wdB = [Buf("wd0"), Buf("wd1")]
            wn = 0
            for unit in units:
                otl = []
                pos = 0
                q = 0
                for (o0, o1, lo, hi) in unit:
                    sc0, sc1 = max(o0 - 1, lo), min(o1 + 1, hi)
                    tis = [i for i, (a_, b_) in enumerate(TILES) if a_ < sc1 and b_ > sc0]
                    T.dma_op("sp", h2u[:, :, pos:pos + sc1 - sc0], h2T_d[:, :, sc0:sc1], reads=tuple(TB["h2"][i] for i in tis), writes=(h2B,))
                    nt_ = (o1 - o0 + 509) // 510
                    tw = (o1 - o0 + nt_ - 1) // nt_
                    for a0 in range(o0, o1, tw):
                        a1 = min(o1, a0 + tw)
                        c0, c1 = max(a0 - 1, lo), min(a1 + 1, hi)
                        otl.append((a0, a1, c0, c1, pos + c0 - sc0, q))
                        q += a1 - a0
                    pos += sc1 - sc0
                nout = q

                def load_up(j):
                    load_w(wup[:, j % 3], W["w_up"][l, j], wupB[j % 3])

                load_up(0)
                load_up(1)
                banks = {}
                for j in range(NJ + 1):
                    if len(otl) <= 2:
                        ada_tick()
                    if j < NJ and j + 2 < NJ:
                        load_up(j + 2)
                    jb3 = j % 3
                    jj = j - 1
                    jb = jj % 2
                    for oi, (a0, a1, c0, c1, hp, q0) in enumerate(otl):
                        if j < NJ:
                            for g in range(2):
                                pm, pmB = psum("all")
                                T.mm(pmB, pm[:, :c1 - c0], [(wup[:, jb3, k, g * 128:(g + 1) * 128], h2u[:, k, hp:hp + c1 - c0]) for k in range(KD)],
                                     reads=(wupB[jb3], h2B))
                                banks[(j, oi, g)] = (pm, pmB)
                        if j >= 1:
                            no = a1 - a0
                            b0 = a0 - c0
                            lo_ = 0 if c0 < a0 else 1
                            hi_ = no if c1 > a1 else no - 1
                            for g in range(2):
                                pm, pmB = banks.pop((jj, oi, g))
                                cwp = lambda k, g=g, jj=jj: cst["ffn_cw"][:, l, jj, g, k:k + 1]
                                T.op("act", lambda e, pm=pm, g=g, jj=jj, jb=jb, b0=b0, no=no, q0=q0: e.activation(
                                    out=cu[:, jb, g, q0:q0 + no], in_=pm[:, b0:b0 + no], func=AF.Identity,
                                    scale=cwp(1), bias=cst["ffn_cb"][:, l, jj, g:g + 1]), reads=(pmB, cstB), writes=(cuB[jb][g][oi],))
                                T.op("dve", lambda e, pm=pm, g=g, jb=jb, b0=b0, no=no, q0=q0, lo_=lo_: e.scalar_tensor_tensor(
                                    out=cu[:, jb, g, q0 + lo_:q0 + no], in0=pm[:, b0 + lo_ - 1:b0 + no - 1], scalar=cwp(0),
                                    in1=cu[:, jb, g, q0 + lo_:q0 + no], op0=ALU.mult, op1=ALU.add), reads=(pmB, cstB, cuB[jb][g][oi]), writes=(cuB[jb][g][oi],))
                                T.op("dve", lambda e, pm=pm, g=g, jb=jb, b0=b0, no=no, q0=q0, hi_=hi_: e.scalar_tensor_tensor(
                                    out=cu[:, jb, g, q0:q0 + hi_], in0=pm[:, b0 + 1:b0 + hi_ + 1], scalar=cwp(2),
                                    in1=cu[:, jb, g, q0:q0 + hi_], op0=ALU.mult, op1=ALU.add), reads=(pmB, cstB, cuB[jb][g][oi]), writes=(cuB[jb][g][oi],))
                    if j >= 1:
                        T.op("act", lambda e, jb=jb: e.activation(out=cu[:, jb, 1, :nout], in_=cu[:, jb, 1, :nout], func=AF.Gelu_apprx_tanh),
                             reads=tuple(cuB[jb][1]), writes=tuple(cuB[jb][1]))
                        T.op("pool", lambda e, jb=jb, jj=jj: e.tensor_tensor(out=a_sb[:, jj, :nout], in0=cu[:, jb, 0, :nout], in1=cu[:, jb, 1, :nout], op=ALU.mult),
                             reads=tuple(cuB[jb][0]) + tuple(cuB[jb][1]), writes=(aB[jj],))
                def load_dn(oc, wb):
                    load_w(wd[:, wb], W["w_down"][l, oc], wdB[wb])
                load_dn(0, wn % 2)
                for oc in range(KD):
                    wb = wn % 2
                    wn += 1
                    if oc + 1 < KD:
                        load_dn(oc + 1, wn % 2)
                    for (a0, a1, c0, c1, hp, q0) in otl:
                        tis = [i for i, (a_, b_) in enumerate(TILES) if a_ < a1 and b_ > a0]
                        col = 1 if tis[0] == 0 else 0
                        pm, pmB = psum("all")
                        T.mm(pmB, pm[:, :a1 - a0], [(wd[:, wb, j, :], a_sb[:, j, q0:q0 + a1 - a0]) for j in range(NJ)],
                             reads=(wdB[wb],) + tuple(aB))
                        T.op("dve", lambda e, pm=pm, oc=oc, a0=a0, a1=a1, col=col: e.scalar_tensor_tensor(
                            out=xT[:, oc, a0:a1], in0=pm[:, :a1 - a0], scalar=lay2[:, l % 2, 5, oc, col:col + 1], in1=xT[:, oc, a0:a1],
                            op0=ALU.mult, op1=ALU.add), reads=(pmB, layB2[l % 2]) + tuple(XB[i] for i in tis), writes=tuple(XB[i] for i in tis))
                if l + 1 < n_layers:
                    ui = units.index(unit)
                    if ui == 0:
                        while ada_t[0] < 13:
                            ada_tick()
                        layer_consts(l + 1)
                        norm_modulate(l + 1, 0, hT_d, TB["h"], rngs=[(0, 256), (256, 766)])
                    elif ui == 1:
                        norm_modulate(l + 1, 0, hT_d, TB["h"], rngs=[(766, 1278), (1278, 1534)])
                    else:
                        norm_modulate(l + 1, 0, hT_d, TB["h"], rngs=[(1534, 2046), (2046, 2304)])
        T.barrier()

    def final_norm():
        with sbt("fn_sq", [128, 2, 512], BF16) as sq_h, sbt("fn_rs", [128, 2, 512], F32) as rs_h, \
                sbt("fn_o", [128, 2, KD, 512], F32) as o_h:
            sq, rs, ost = sq_h.ap(), rs_h.ap(), o_h.ap()
            sqB = [Buf("fsq0"), Buf("fsq1")]
            rsB = [Buf("frs0"), Buf("frs1")]
            oB = [Buf("fo0"), Buf("fo1")]
            n = 0
            for ti, (t0, t1) in enumerate(TILES):
                if ti == 0:
                    continue
                w = t1 - t0
                pm, pmB = psum("mm")
                for k in range(KD):
                    b = n % 2
                    n += 1
                    T.op("act", lambda e, k=k, b=b: e.activation(out=sq[:, b, :w], in_=xT[:, k, t0:t1], func=AF.Square), reads=(XB[ti],), writes=(sqB[b],))
                    for d in T._deps((sqB[b], cstB), (pmB,) if k == 0 else ()):
                        if not d[0].startswith("pe_"):
                            T._wait("pe", d)
                    ins = nc.tensor.matmul(pm[:, :w], lhsT=ones_bf, rhs=sq[:, b, :w], start=(k == 0), stop=(k == KD - 1))
                    dep = T._bump("pe")
                    ins.then_inc(dep[1], 1)
                    T._record(dep, (sqB[b], cstB), (pmB,))
                rb = ti % 2
                T.op("act", lambda e: e.activation(out=rs[:, rb, :w], in_=pm[:, :w], func=AF.Ln, scale=1.0 / D, bias=eps_t[:, 0:1]), reads=(pmB, cstB), writes=(rsB[rb],))
                T.op("act", lambda e: e.activation(out=rs[:, rb, :w], in_=rs[:, rb, :w], func=AF.Exp, scale=-0.5), reads=(rsB[rb],), writes=(rsB[rb],))
                for k in range(KD):
                    T.op("dve", lambda e, k=k: e.scalar_tensor_tensor(out=ost[:, rb, k, :w], in0=xT[:, k, t0:t1], scalar=cst["final_norm"][:, k:k + 1],
                                                                   in1=rs[:, rb, :w], op0=ALU.mult, op1=ALU.mult), reads=(XB[ti], cstB, rsB[rb]), writes=(oB[rb],))
                T.dma_op("sp", out_T[:, :, t0 - C:t1 - C], ost[:, rb, :, :w], reads=(oB[rb],))

    done_all = False
    for l in range(n_layers):
        last = (l == L - 1)
        if l == 0:
            layer_consts(l)
            norm_modulate(l, 0, hT_d, TB["h"])
            T.barrier()
        if upto == "norm":
            break
        attention(l, last)
        if upto == "attn":
            break
        fourier(l, last)
        if upto == "fourier":
            break
        rnn(l, last)
        if upto == "rnn":
            break
        merge(l, last)
        merge2_and_norm(l, last)
        if upto == "merge":
            break
        ffn(l, last)
        done_all = (l == n_layers - 1)

    if debug:
        for i, (t0, t1) in enumerate(TILES):
            T.dma_op("sp", dbg_x[:, :, t0:t1], xT[:, :, t0:t1], reads=(XB[i],))
    if done_all and n_layers == L:
        final_norm()
    else:
        for i, (t0, t1) in enumerate(TILES[1:]):
            T.dma_op("sp", out_T[:, :, t0 - C:t1 - C], xT[:, :, t0:t1], reads=(XB[i + 1],))
    T.wait_all("sp")
    print(f"[build] instrs={T.n_ins} waits={T.n_wait}")
    return nc


_CACHE = {}


def kernel(**inputs):
    inp = {k: np.asarray(v) for k, v in inputs.items()}
    shared = prep_shared(inp)
    nc = build()
    in_maps = []
    for b in range(8):
        m = dict(shared)
        m.update(prep_core(inp, b))
        in_maps.append(m)
    res = run_bass_kernel_spmd(nc, in_maps, core_ids=list(range(8)))
    out = np.empty((8, S, D), np.float32)
    for b in range(8):
        oT = res.results[b]["out_T"]
        out[b] = oT.transpose(2, 1, 0).reshape(S, D)
    return out
```

```python
import os
import math
import numpy as np
import ml_dtypes
import concourse.bass as bass
import concourse.mybir as mybir
from concourse.bass_utils import run_bass_kernel_spmd

F32 = mybir.dt.float32
BF16 = mybir.dt.bfloat16
AF = mybir.ActivationFunctionType
ALU = mybir.AluOpType

L = 4
D = 1024
KD = 8
S = 2048
C = 256
NT = S + C
QL, KVL, ROPE = 384, 256, 32
NH = 8
FW = 512
RW = 512
DFF = 2816
NJ = DFF // 128
EPS = 1e-6
TILES = [(0, 256), (256, 768), (768, 1280), (1280, 1792), (1792, 2304)]


class Buf:
    __slots__ = ("name", "w", "r", "psum")

    def __init__(self, name, psum=False):
        self.name = name
        self.w = None
        self.r = {}
        self.psum = psum


class Trk:
    LIMIT = 30000

    def __init__(self, nc, n_dma=12):
        self.nc = nc
        self.eng = {}
        for name, obj in (("pe", nc.tensor), ("act", nc.scalar), ("dve", nc.vector),
                          ("pool", nc.gpsimd), ("sp", nc.sync)):
            self.eng[name] = {"obj": obj, "sem": nc.alloc_semaphore(f"s_{name}_0"), "cnt": 0, "ep": 0,
                              "key": f"{name}_0"}
        self.waited = {name: {} for name in self.eng}
        self.dma = {}
        for q in ("sp", "pool"):
            self.dma[q] = {"slots": [{"sem": nc.alloc_semaphore(f"d_{q}_{i}"), "cum": 0, "key": f"d_{q}_{i}"}
                                     for i in range(n_dma)], "next": 0}
        self.n_wait = 0
        self.n_ins = 0

    def _wait(self, eng, dep):
        key, sem, val = dep
        w = self.waited[eng]
        if w.get(key, 0) >= val:
            return
        self.eng[eng]["obj"].wait_ge(sem, val)
        w[key] = val
        self.n_wait += 1

    def _deps(self, reads, writes):
        deps = {}

        def add(d):
            if d is None:
                return
            if d[0] not in deps or deps[d[0]][2] < d[2]:
                deps[d[0]] = d
        for b in reads:
            add(b.w)
            if b.psum:
                for d in b.r.values():
                    add(d)
        for b in writes:
            add(b.w)
            for d in b.r.values():
                add(d)
        return deps.values()

    def _record(self, dep, reads, writes):
        for b in reads:
            b.r[dep[0]] = dep
        for b in writes:
            b.w = dep
            b.r = {}

    def _bump(self, eng):
        E = self.eng[eng]
        if E["cnt"] >= self.LIMIT:
            E["ep"] += 1
            E["sem"] = self.nc.alloc_semaphore(f"s_{eng}_{E['ep']}")
            E["cnt"] = 0
            E["key"] = f"{eng}_{E['ep']}"
        E["cnt"] += 1
        return (E["key"], E["sem"], E["cnt"])

    def op(self, eng, fn, reads=(), writes=()):
        for d in self._deps(reads, writes):
            if eng == "pe" and d[0].startswith("pe_"):
                continue
            self._wait(eng, d)
        ins = fn(self.eng[eng]["obj"])
        dep = self._bump(eng)
        ins.then_inc(dep[1], 1)
        self._record(dep, reads, writes)
        self.n_ins += 1
        return ins

    def mm(self, out_buf, out_ap, pairs, reads):
        for d in self._deps(reads, (out_buf,)):
            if d[0].startswith("pe_"):
                continue
            self._wait("pe", d)
        n = len(pairs)
        pe = self.eng["pe"]["obj"]
        for i, (lt, rh) in enumerate(pairs):
            ins = pe.matmul(out_ap, lhsT=lt, rhs=rh, start=(i == 0), stop=(i == n - 1))
            self.n_ins += 1
        dep = self._bump("pe")
        ins.then_inc(dep[1], 1)
        self._record(dep, reads, (out_buf,))

    def dma_op(self, q, out, in_, reads=(), writes=(), **kw):
        for d in self._deps(reads, writes):
            self._wait(q, d)
        Q = self.dma[q]
        slot = Q["slots"][Q["next"]]
        Q["next"] = (Q["next"] + 1) % len(Q["slots"])
        if slot["cum"] > 0:
            self._wait(q, (slot["key"], slot["sem"], slot["cum"]))
        if slot["cum"] >= self.LIMIT:
            slot["sem"] = self.nc.alloc_semaphore(slot["key"] + "n")
            slot["key"] = slot["key"] + "n"
            slot["cum"] = 0
        ins = self.eng[q]["obj"].dma_start(out=out, in_=in_, **kw)
        slot["cum"] += 16
        ins.then_inc(slot["sem"], 16)
        dep = (slot["key"], slot["sem"], slot["cum"])
        self._record(dep, reads, writes)
        self.n_ins += 1
        return dep

    def barrier(self):
        fence = []
        for name, E in self.eng.items():
            if E["cnt"] > 0:
                fence.append((E["key"], E["sem"], E["cnt"]))
        for q in self.dma.values():
            for s in q["slots"]:
                if s["cum"] > 0:
                    fence.append((s["key"], s["sem"], s["cum"]))
        for name in self.eng:
            for d in fence:
                if d[0].startswith(name + "_"):
                    continue
                self._wait(name, d)

    def wait_all(self, eng):
        for q in self.dma.values():
            for s in q["slots"]:
                if s["cum"] > 0:
                    self._wait(eng, (s["key"], s["sem"], s["cum"]))
        for name, E in self.eng.items():
            if name != eng and E["cnt"] > 0:
                self._wait(eng, (E["key"], E["sem"], E["cnt"]))


def _pk(w):
    K, N = w.shape
    return np.ascontiguousarray(w.reshape(K // 128, 128, N).transpose(1, 0, 2))


def _vec(v):
    sh = v.shape
    return np.ascontiguousarray(np.moveaxis(v.reshape(sh[:-1] + (sh[-1] // 128, 128)), -1, 0))


def _tables():
    t = {}
    n_rows = S // 64
    row = np.repeat(np.arange(n_rows, dtype=np.float32), 64)
    col = np.tile(np.arange(64, dtype=np.float32), n_rows)
    nf = ROPE // 4
    inv = (np.float32(10000.0) ** (-np.arange(nf, dtype=np.float32) / nf)).astype(np.float32)
    ang = np.concatenate([row[:, None] * inv, col[:, None] * inv], axis=-1).astype(np.float32)
    cos = np.cos(ang).astype(np.float32).T
    sin = np.sin(ang).astype(np.float32).T
    t["rope_cos"] = np.ascontiguousarray(np.concatenate([cos, cos], 0))
    t["rope_sin"] = np.ascontiguousarray(np.concatenate([-sin, sin], 0))
    bf = ml_dtypes.bfloat16

    def dft(n):
        k = np.arange(n, dtype=np.int64)
        a = (np.outer(k, k) % n).astype(np.float64) * (2.0 * np.pi / n)
        return np.cos(a), np.sin(a)
    cc, sc = dft(128)
    t["dft_c"] = np.ascontiguousarray(np.concatenate([cc, sc], 1).astype(bf))
    cl, sl = dft(S)
    t["dft_lc"] = np.ascontiguousarray(cl.reshape(S // 128, 128, S).transpose(1, 0, 2).astype(bf))
    t["dft_ls"] = np.ascontiguousarray((-sl).reshape(S // 128, 128, S).transpose(1, 0, 2).astype(bf))
    c2, s2 = dft(C)
    t["dft_cc"] = np.ascontiguousarray(c2.reshape(C // 128, 128, C).transpose(1, 0, 2).astype(bf))
    t["dft_cs"] = np.ascontiguousarray((-s2).reshape(C // 128, 128, C).transpose(1, 0, 2).astype(bf))
    return t


def prep_shared(inp):
    f = np.float32
    sh = {}
    w_in = inp["w_in"]
    o_cq, o_ckv, o_kr, o_uf, o_ur, o_ug, o_gl = 0, 384, 640, 672, 1184, 1696, 2208
    kr_cols = np.arange(o_kr, o_kr + 32)
    kr_sw = np.concatenate([kr_cols[16:], kr_cols[:16]])
    cols_a = np.concatenate([np.arange(0, 672), np.arange(o_ckv + 192, o_ckv + 256), kr_sw])
    sh["w_in_a"] = np.stack([_pk(w_in[l][:, cols_a]) for l in range(L)])
    sh["w_in_f"] = np.stack([_pk(w_in[l][:, o_uf:o_uf + 512]) for l in range(L)])
    sh["w_in_r"] = np.stack([_pk(w_in[l][:, o_ur:o_ur + 512]) for l in range(L)])
    sh["w_in_g"] = np.stack([_pk(w_in[l][:, o_ug:o_ug + 512]) for l in range(L)])
    sh["w_in_gl"] = np.stack([_pk(w_in[l][:, o_gl:o_gl + 3072]) for l in range(L)])
    w_uq = inp["w_uq"]
    sh["w_uq_a"] = np.stack([_pk(w_uq[l].reshape(QL, NH * 96)) for l in range(L)])
    perm = np.concatenate([np.arange(64), np.arange(80, 96), np.arange(64, 80)])
    sh["w_uq_b"] = np.stack([_pk(w_uq[l][:, :, perm].reshape(QL, NH * 96)) for l in range(L)])
    w_ukv = inp["w_ukv"]
    sh["w_ukv_k"] = np.stack([_pk(np.ascontiguousarray(w_ukv[l][:, :, :64]).reshape(KVL, NH * 64)) for l in range(L)])
    sh["w_ukv_v"] = np.stack([_pk(np.ascontiguousarray(w_ukv[l][:, :, 64:]).reshape(KVL, NH * 64)) for l in range(L)])
    for nm in ("w_o_attn", "w_o_fourier", "w_o_rnn", "w_out"):
        sh[nm] = np.stack([_pk(inp[nm][l]) for l in range(L)])
    sh["w_down"] = np.stack([np.ascontiguousarray(_pk(inp["w_down"][l]).reshape(128, NJ, KD, 128).transpose(2, 0, 1, 3)) for l in range(L)])
    w_up = inp["w_up"]
    wu = np.empty((L, NJ, 128, KD, 256), f)
    for l in range(L):
        a = _pk(w_up[l])
        for j in range(NJ):
            wu[l, j, :, :, :128] = a[:, :, j * 128:(j + 1) * 128]
            wu[l, j, :, :, 128:] = a[:, :, DFF + j * 128:DFF + (j + 1) * 128]
    sh["w_up"] = wu
    sh["w_ada"] = np.stack([_pk(inp["w_ada"][l]) for l in range(L)])
    sh["b_ada"] = _vec(inp["b_ada"])
    sh["norm_mix"] = _vec(inp["norm_mix"])
    sh["norm_ffn"] = _vec(inp["norm_ffn"])
    sh["q_norm"] = _vec(inp["q_norm"])
    sh["kv_norm"] = _vec(inp["kv_norm"])
    sh["final_norm"] = _vec(inp["final_norm"])
    sh["rnn_cw"] = _vec(inp["rnn_conv_w"])
    sh["rnn_cb"] = _vec(inp["rnn_conv_b"])
    sh["rg_lam"] = _vec(inp["rg_lambda"])
    sh["rg_ba"] = _vec(inp["rg_b_a"])
    sh["rg_bx"] = _vec(inp["rg_b_x"])
    rg = np.zeros((L, 128, 4, 4, 128), f)
    for l in range(L):
        for c in range(4):
            for ty, (nm, dr) in enumerate((("rg_w_a", 0), ("rg_w_x", 0), ("rg_w_a", 1), ("rg_w_x", 1))):
                for hh in range(2):
                    rg[l, hh * 64:(hh + 1) * 64, c, ty, hh * 64:(hh + 1) * 64] = inp[nm][l, dr, 2 * c + hh]
    sh["rg_w"] = rg
    fcw = inp["ffn_conv_w"]
    fcb = inp["ffn_conv_b"]
    cw = np.empty((128, L, NJ, 2, 3), f)
    cb = np.empty((128, L, NJ, 2), f)
    for l in range(L):
        for j in range(NJ):
            for g in range(2):
                sl = slice(g * DFF + j * 128, g * DFF + (j + 1) * 128)
                cw[:, l, j, g, :] = fcw[l][:, sl].T
                cb[:, l, j, g] = fcb[l][sl]
    sh["ffn_cw"] = cw
    sh["ffn_cb"] = cb
    sh.update(_tables())
    return {k: np.ascontiguousarray(v) for k, v in sh.items()}


def prep_core(inp, b):
    xc = np.concatenate([inp["ctx"][b], inp["x"][b]], axis=0)
    xT0 = np.ascontiguousarray(xc.T.reshape(KD, 128, NT).transpose(1, 0, 2))
    cv = np.stack([inp["c"][b], inp["c_ctx"]], axis=-1)
    cvec = np.ascontiguousarray(cv.reshape(KD, 128, 2).transpose(1, 0, 2))
    return {"xT0": xT0, "cvec": cvec}


def build(n_layers=L, debug=False, upto="all"):
    nc = bass.Bass("TRN2", target_bir_lowering=False)
    T = Trk(nc)

    def din(name, shape, dt=F32):
        return nc.dram_tensor(name, list(shape), dt, kind="ExternalInput").ap()

    xT0 = din("xT0", (128, KD, NT))
    cvec = din("cvec", (128, KD, 2))
    W = {}
    for name, shape in (("w_in_a", (L, 128, KD, 768)), ("w_in_f", (L, 128, KD, 512)), ("w_in_r", (L, 128, KD, 512)),
                        ("w_in_g", (L, 128, KD, 512)), ("w_in_gl", (L, 128, KD, 3072)),
                        ("w_uq_a", (L, 128, 3, 768)), ("w_uq_b", (L, 128, 3, 768)),
                        ("w_ukv_k", (L, 128, 2, 512)), ("w_ukv_v", (L, 128, 2, 512)),
                        ("w_o_attn", (L, 128, 4, D)), ("w_o_fourier", (L, 128, 4, D)), ("w_o_rnn", (L, 128, 4, D)),
                        ("w_out", (L, 128, KD, D)), ("w_down", (L, KD, 128, NJ, 128)), ("w_up", (L, NJ, 128, KD, 256)),
                        ("w_ada", (L, 128, KD, 6 * D)), ("b_ada", (128, L, 48)), ("norm_mix", (128, L, 8)),
                        ("norm_ffn", (128, L, 8)), ("q_norm", (128, L, 3)), ("kv_norm", (128, L, 2)),
                        ("final_norm", (128, 8)), ("rnn_cw", (128, L, 4, 4)), ("rnn_cb", (128, L, 4)),
                        ("rg_lam", (128, L, 2, 4)), ("rg_ba", (128, L, 2, 4)), ("rg_bx", (128, L, 2, 4)),
                        ("rg_w", (L, 128, 4, 4, 128)), ("ffn_cw", (128, L, NJ, 2, 3)), ("ffn_cb", (128, L, NJ, 2)),
                        ("rope_cos", (32, S)), ("rope_sin", (32, S))):
        W[name] = din(name, shape)
    for name, shape in (("dft_c", (128, 256)), ("dft_lc", (128, 16, S)), ("dft_ls", (128, 16, S)),
                        ("dft_cc", (128, 2, C)), ("dft_cs", (128, 2, C))):
        W[name] = din(name, shape, BF16)

    out_T = nc.dram_tensor("out_T", [128, KD, S], F32, kind="ExternalOutput").ap()

    kind_s = "ExternalOutput" if debug else "Internal"

    def dscr(name, shape, dt=BF16):
        return nc.dram_tensor(name, list(shape), dt, kind=kind_s).ap()

    hT_d = dscr("hT_d", (128, KD, NT))
    attT_d = dscr("attT_d", (128, 4, NT))
    yfT_d = dscr("yfT_d", (128, 4, NT))
    hgT_d = dscr("hgT_d", (128, 4, NT))
    mT_d = dscr("mT_d", (128, KD, NT))
    h2T_d = dscr("h2T_d", (128, KD, NT))
    dbg_x = dscr("dbg_x", (128, KD, NT), F32) if debug else None
    TB = {nm: [Buf(f"{nm}{i}") for i in range(len(TILES))] for nm in ("h", "att", "yf", "hg", "m", "h2")}

    def sb(name, shape, dt=F32):
        return nc.alloc_sbuf_tensor(name, list(shape), dt).ap()

    uid = [0]

    def sbt(name, shape, dt):
        uid[0] += 1
        return nc.sbuf_tensor(f"{name}_{uid[0]}", list(shape), dt)

    xT = sb("xT", (128, KD, NT))
    XB = [Buf(f"x{i}") for i in range(len(TILES))]
    mod = sb("mod", (128, L, 48, 2))
    modB = Buf("mod")
    cst = {}
    cstB = Buf("cst")
    for name in ("b_ada", "norm_mix", "norm_ffn", "q_norm", "kv_norm", "final_norm", "rnn_cw", "rnn_cb",
                 "rg_lam", "rg_ba", "rg_bx", "ffn_cw", "ffn_cb"):
        cst[name] = sb("c_" + name, W[name].shape)
    cl_t = sb("cl_t", (128, L, 2, 4))
    hcl_t = sb("hcl_t", (128, L, 2, 4))
    eps_t = sb("eps_t", (128, 1))
    hba_t = sb("hba_t", (128, L, 2, 4))
    hbx_t = sb("hbx_t", (128, L, 2, 4))
    ones_bf = sb("ones_bf", (128, 128), BF16)
    silu_c = sb("silu_c", (128, KD, 2), BF16)
    lay2 = sb("lay", (128, 2, 6, KD, 2))
    layB2 = [Buf("lay0"), Buf("lay1")]

    PS = [nc.alloc_psum_tensor(f"ps{i}", [128, 512], F32).ap() for i in range(8)]
    PSB = [Buf(f"ps{i}", psum=True) for i in range(8)]
    ps_rr = {"mm": [2, 3, 4, 5, 6, 7], "acc": [0, 1], "s": [2, 3, 4], "p": [5, 6, 7], "all": [0, 1, 2, 3, 4, 5, 6, 7]}
    ps_i = {k: 0 for k in ps_rr}

    def psum(pool="mm"):
        lst = ps_rr[pool]
        i = lst[ps_i[pool] % len(lst)]
        ps_i[pool] += 1
        return PS[i], PSB[i]

    CUT = int(os.environ.get("KCUT", "99"))
    for name in cst:
        T.dma_op("sp", cst[name], W[name], writes=(cstB,))
    for i, (t0, t1) in enumerate(TILES):
        T.dma_op("sp", xT[:, :, t0:t1], xT0[:, :, t0:t1], writes=(XB[i],))
    T.op("dve", lambda e: e.memset(ones_bf, 1.0), writes=(cstB,))
    T.op("dve", lambda e: e.memset(eps_t, EPS), writes=(cstB,))
    if CUT <= 0:
        n_layers = 0
    T.op("act", lambda e: e.activation(out=cl_t, in_=cst["rg_lam"], func=AF.Exp, scale=-1.0), reads=(cstB,), writes=(cstB,))
    T.op("act", lambda e: e.activation(out=cl_t, in_=cl_t, func=AF.Ln, bias=1.0), reads=(cstB,), writes=(cstB,))
    T.op("dve", lambda e: e.tensor_scalar(out=cl_t, in0=cl_t, scalar1=-8.0, scalar2=None, op0=ALU.mult), reads=(cstB,), writes=(cstB,))
    T.op("dve", lambda e: e.tensor_scalar(out=hcl_t, in0=cl_t, scalar1=0.5, scalar2=None, op0=ALU.mult), reads=(cstB,), writes=(cstB,))
    T.op("dve", lambda e: e.tensor_scalar(out=hba_t, in0=cst["rg_ba"], scalar1=0.5, scalar2=None, op0=ALU.mult), reads=(cstB,), writes=(cstB,))
    T.op("dve", lambda e: e.tensor_scalar(out=hbx_t, in0=cst["rg_bx"], scalar1=0.5, scalar2=None, op0=ALU.mult), reads=(cstB,), writes=(cstB,))
    cv_sb = sb("cv_sb", (128, KD, 2))
    T.dma_op("sp", cv_sb, cvec, writes=(cstB,))
    T.op("act", lambda e: e.activation(out=silu_c, in_=cv_sb, func=AF.Silu), reads=(cstB,), writes=(cstB,))

    if CUT <= 1:
        n_layers = 0
    with sbt("wada", [128, 2, KD, 512], BF16) as wada_h:
        wada = wada_h.ap()
        wadaB = [Buf("wada0"), Buf("wada1")]
        gi = 0
        for l in range(min(1, n_layers)):
            pm, pmB = psum("acc")
            for g in range(12):
                bi = gi % 2
                gi += 1
                T.dma_op("pool", wada[:, bi], W["w_ada"][l, :, :, g * 512:(g + 1) * 512], writes=(wadaB[bi],))
                for ff in range(4):
                    f = g * 4 + ff
                    T.mm(pmB, pm[:, 2 * f:2 * f + 2],
                         [(wada[:, bi, k, ff * 128:(ff + 1) * 128], silu_c[:, k, :]) for k in range(KD)],
                         reads=(wadaB[bi], cstB))
            T.op("dve", lambda e: e.tensor_tensor(
                out=mod[:, l], in0=pm[:, 0:96].rearrange("p (f c) -> p f c", c=2),
                in1=cst["b_ada"][:, l, :].unsqueeze(2).to_broadcast([128, 48, 2]), op=ALU.add),
                reads=(pmB, cstB), writes=(modB,))
    T.barrier()
    if CUT <= 2:
        n_layers = 0

    def load_w(dst, src, buf, q="pool"):
        T.dma_op(q, dst, src, writes=(buf,))

    def norm_modulate(l, which, dst_d, dstB, pre=None, tiles=None, rngs=None):
        i_sh, i_gs = (0, 1) if which == 0 else (3, 4)
        lay, layB = lay2[:, l % 2], layB2[l % 2]
        if rngs is None:
            tl_ = list(range(len(TILES))) if tiles is None else list(tiles)
            rngs = [TILES[ti] for ti in tl_]
        else:
            tl_ = list(range(len(rngs)))
        ovl = lambda t0, t1: [i for i, (a_, b_) in enumerate(TILES) if a_ < t1 and b_ > t0]
        with sbt("nm_sq", [128, 2, 512], BF16) as sq_h, \
                sbt("nm_rs", [128, 2, 512], F32) as rs_h, \
                sbt("nm_tmp", [128, 2, 512], F32) as tmp_h, \
                sbt("nm_h", [128, 2, KD, 512], BF16) as h_h:
            sq, rs, tmp, hh = sq_h.ap(), rs_h.ap(), tmp_h.ap(), h_h.ap()
            sqB = [Buf("sq0"), Buf("sq1")]
            rsB = [Buf("rs0"), Buf("rs1")]
            tmpB = [Buf("tmp0"), Buf("tmp1")]
            hB = [Buf("hh0"), Buf("hh1")]
            nn = [0]
            pend = {}

            def stage1(i_):
                t0, t1 = rngs[i_]
                w = t1 - t0
                xb = tuple(XB[i] for i in ovl(t0, t1))
                pm, pmB = psum("acc")
                for k in range(KD):
                    b = nn[0] % 2
                    nn[0] += 1
                    T.op("act", lambda e, k=k, b=b: e.activation(out=sq[:, b, :w], in_=xT[:, k, t0:t1], func=AF.Square),
                         reads=xb, writes=(sqB[b],))
                    for d in T._deps((sqB[b], cstB), (pmB,) if k == 0 else ()):
                        if not d[0].startswith("pe_"):
                            T._wait("pe", d)
                    ins = nc.tensor.matmul(pm[:, :w], lhsT=ones_bf, rhs=sq[:, b, :w], start=(k == 0), stop=(k == KD - 1))
                    dep = T._bump("pe")
                    ins.then_inc(dep[1], 1)
                    T._record(dep, (sqB[b], cstB), (pmB,))
                pend[i_] = (pm, pmB)

            def stage2(i_):
                t0, t1 = rngs[i_]
                w = t1 - t0
                col = 1 if t0 < C else 0
                xb = tuple(XB[i] for i in ovl(t0, t1))
                pm, pmB = pend.pop(i_)
                rb = i_ % 2
                T.op("act", lambda e: e.activation(out=rs[:, rb, :w], in_=pm[:, :w], func=AF.Ln, scale=1.0 / D, bias=eps_t[:, 0:1]),
                     reads=(pmB, cstB), writes=(rsB[rb],))
                T.op("act", lambda e: e.activation(out=rs[:, rb, :w], in_=rs[:, rb, :w], func=AF.Exp, scale=-0.5), reads=(rsB[rb],), writes=(rsB[rb],))
                for k in range(KD):
                    b = k % 2
                    T.op("dve", lambda e, k=k, b=b: e.scalar_tensor_tensor(
                        out=tmp[:, b, :w], in0=xT[:, k, t0:t1], scalar=lay[:, i_gs, k, col:col + 1], in1=rs[:, rb, :w],
                        op0=ALU.mult, op1=ALU.mult), reads=xb + (layB, rsB[rb]), writes=(tmpB[b],))
                    T.op("act", lambda e, k=k, b=b: e.activation(
                        out=hh[:, rb, k, :w], in_=tmp[:, b, :w], func=AF.Identity, bias=lay[:, i_sh, k, col:col + 1]),
                        reads=(tmpB[b], layB), writes=(hB[rb],))
                T.dma_op("sp", dst_d[:, :, t0:t1], hh[:, rb, :, :w], reads=(hB[rb],), writes=tuple(dstB[i] for i in ovl(t0, t1)))

            if pre is not None:
                pre(tl_[0])
            stage1(0)
            for i_ in range(len(rngs)):
                if i_ + 1 < len(rngs):
                    if pre is not None:
                        pre(tl_[i_ + 1])
                    stage1(i_ + 1)
                stage2(i_)

    def layer_consts(l):
        lay, layB = lay2[:, l % 2], layB2[l % 2]
        for idx, ch in ((0, 0), (2, 2), (3, 3), (5, 5)):
            T.op("dve", lambda e, idx=idx, ch=ch: e.tensor_copy(out=lay[:, idx], in_=mod[:, l, ch * 8:(ch + 1) * 8, :]),
                 reads=(modB,), writes=(layB,))
        for idx, ch, nm in ((1, 1, "norm_mix"), (4, 4, "norm_ffn")):
            T.op("dve", lambda e, idx=idx, ch=ch, nm=nm: e.scalar_tensor_tensor(
                out=lay[:, idx], in0=mod[:, l, ch * 8:(ch + 1) * 8, :], scalar=1.0,
                in1=cst[nm][:, l, :].unsqueeze(2).to_broadcast([128, KD, 2]), op0=ALU.add, op1=ALU.mult),
                reads=(modB, cstB), writes=(layB,))

    def attention(l, last):
        with sbt("a_cqn", [128, 3, NT], BF16) as cqn_h, sbt("a_ckvn", [128, 2, NT], BF16) as ckvn_h, \
                sbt("a_kr", [96, NT], BF16) as kr_h, \
                sbt("a_wuq", [128, 2, 3, 768], BF16) as wuq_h, sbt("a_wukv", [128, 2, 2, 512], BF16) as wukv_h, \
                sbt("a_rope", [96, 2, 2, 512], F32) as rope_h, sbt("a_t", [128, 4, 512], F32) as t_h, \
                sbt("a_rd", [128, 2, 512], F32) as rd_h:
            cqn, ckvn, krT = cqn_h.ap(), ckvn_h.ap(), kr_h.ap()
            wuq, wukv = wuq_h.ap(), wukv_h.ap()
            rope, tt, rd = rope_h.ap(), t_h.ap(), rd_h.ap()
            cqnB = [Buf(f"cqn{i}") for i in range(5)]
            ckvnB = [Buf(f"ckvn{i}") for i in range(5)]
            krB = [Buf(f"kr{i}") for i in range(5)]
            VB = [Buf(f"v{i}") for i in range(18)]
            wB = Buf("aw")
            kB = [[Buf(f"k{j}_{i}") for i in range(5)] for j in range(2)]
            qB = [[Buf(f"q{j}_{i}") for i in range(5)] for j in range(2)]
            EB = [Buf(f"e{i}") for i in range(4)]
            stB = [Buf(f"st{i}") for i in range(5)]
            ropeB = [Buf("rope0"), Buf("rope1")]
            tB = [Buf(f"t{i}") for i in range(4)]
            rdB = [Buf("rd0"), Buf("rd1")]
            rope_n = [0]

            def load_rope(t0, t1):
                b = rope_n[0] % 2
                rope_n[0] += 1
                w = t1 - t0
                T.dma_op("sp", rope[64:96, b, 0, :w], W["rope_cos"][:, t0 - C:t1 - C], writes=(ropeB[b],))
                T.dma_op("sp", rope[64:96, b, 1, :w], W["rope_sin"][:, t0 - C:t1 - C], writes=(ropeB[b],))
                return b

            def apply_rope(dst, dstB_, pa, paB, pb, pbB, rb, w):
                T.op("dve", lambda e: e.tensor_tensor(out=tt[64:96, 0, :w], in0=pa[64:96, :w], in1=rope[64:96, rb, 0, :w], op=ALU.mult),
                     reads=(paB, ropeB[rb]), writes=(tB[0],))
                T.op("dve", lambda e: e.tensor_tensor(out=tt[64:96, 1, :w], in0=pb[64:96, :w], in1=rope[64:96, rb, 1, :w], op=ALU.mult),
                     reads=(pbB, ropeB[rb]), writes=(tB[1],))
                T.op("dve", lambda e: e.tensor_tensor(out=dst, in0=tt[64:96, 0, :w], in1=tt[64:96, 1, :w], op=ALU.add),
                     reads=(tB[0], tB[1]), writes=(dstB_,))

            with sbt("a_win", [128, KD, 768], BF16) as win_h, sbt("a_h", [128, 2, KD, 512], BF16) as ht_h, \
                    sbt("a_raw", [128, 2, 5, 512], F32) as raw_h, sbt("a_sq", [128, 2, 5, 512], BF16) as sq_h:
                win, ht, raw, sq = win_h.ap(), ht_h.ap(), raw_h.ap(), sq_h.ap()
                winB = Buf("win")
                htB = [Buf("ht0"), Buf("ht1")]
                rawB = [[Buf(f"raw{p_}_{i}") for i in range(5)] for p_ in range(2)]
                sqB = [[Buf(f"sq{p_}_{i}") for i in range(5)] for p_ in range(2)]
                load_w(win, W["w_in_a"][l], winB)
                load_w(wukv[:, 1], W["w_ukv_v"][l], wB)
                load_w(wukv[:, 0], W["w_ukv_k"][l], wB)
                load_w(wuq[:, 0], W["w_uq_a"][l], wB)
                load_w(wuq[:, 1], W["w_uq_b"][l], wB)

                def a_load(ti):
                    t0, t1 = TILES[ti]
                    T.dma_op("sp", ht[:, ti % 2, :, :t1 - t0], hT_d[:, :, t0:t1], reads=(TB["h"][ti],), writes=(htB[ti % 2],))

                def a_stage1(ti):
                    t0, t1 = TILES[ti]
                    w = t1 - t0
                    hb = ti % 2
                    if ti + 1 < len(TILES):
                        a_load(ti + 1)
                    for c5 in range(5):
                        pm, pmB = psum("mm")
                        T.mm(pmB, pm[:, :w], [(win[:, k, c5 * 128:(c5 + 1) * 128], ht[:, hb, k, :w]) for k in range(KD)],
                             reads=(winB, htB[hb]))
                        T.op("act", lambda e, c5=c5, pm=pm: e.activation(out=sq[:, hb, c5, :w], in_=pm[:, :w], func=AF.Square),
                             reads=(pmB,), writes=(sqB[hb][c5],))
                        T.op("dve", lambda e, c5=c5, pm=pm: e.tensor_copy(out=raw[:, hb, c5, :w], in_=pm[:, :w]),
                             reads=(pmB,), writes=(rawB[hb][c5],))
                    pa, paB = psum("mm")
                    T.mm(paB, pa[0:96, :w], [(win[:, k, 576:672], ht[:, hb, k, :w]) for k in range(KD)], reads=(winB, htB[hb]))
                    if ti == 0:
                        T.op("act", lambda e, pa=pa: e.activation(out=krT[64:96, t0:t1], in_=pa[64:96, :w], func=AF.Copy),
                             reads=(paB,), writes=(krB[ti],))
                    else:
                        pb, pbB = psum("mm")
                        T.mm(pbB, pb[0:96, :w], [(win[:, k, 672:768], ht[:, hb, k, :w]) for k in range(KD)], reads=(winB, htB[hb]))
                        rb = load_rope(t0, t1)
                        apply_rope(krT[64:96, t0:t1], krB[ti], pa, paB, pb, pbB, rb, w)

                def a_stage2(ti):
                    t0, t1 = TILES[ti]
                    w = t1 - t0
                    hb = ti % 2
                    for (cs, dstn, dstBn, gname, nfeat) in ((range(0, 3), cqn, cqnB, "q_norm", QL), (range(3, 5), ckvn, ckvnB, "kv_norm", KVL)):
                        pm, pmB = psum("mm")
                        T.mm(pmB, pm[:, :w], [(ones_bf, sq[:, hb, c5, :w]) for c5 in cs], reads=tuple(sqB[hb][c5] for c5 in cs) + (cstB,))
                        rb = 0 if nfeat == QL else 1
                        T.op("act", lambda e, pm=pm, rb=rb, nfeat=nfeat: e.activation(out=rd[:, rb, :w], in_=pm[:, :w], func=AF.Ln, scale=1.0 / nfeat, bias=eps_t[:, 0:1]),
                             reads=(pmB, cstB), writes=(rdB[rb],))
                        T.op("act", lambda e, rb=rb: e.activation(out=rd[:, rb, :w], in_=rd[:, rb, :w], func=AF.Exp, scale=-0.5), reads=(rdB[rb],), writes=(rdB[rb],))
                        for ci, c5 in enumerate(cs):
                            T.op("dve", lambda e, ci=ci, c5=c5, rb=rb, dstn=dstn, gname=gname: e.scalar_tensor_tensor(
                                out=dstn[:, ci, t0:t1], in0=raw[:, hb, c5, :w], scalar=cst[gname][:, l, ci:ci + 1], in1=rd[:, rb, :w],
                                op0=ALU.mult, op1=ALU.mult), reads=(rawB[hb][c5], cstB, rdB[rb]), writes=(dstBn[ti],))

                a_load(0)
                a_stage1(0)
                for ti in range(len(TILES)):
                    if ti + 1 < len(TILES):
                        a_stage1(ti + 1)
                    a_stage2(ti)
            T.barrier()
            with sbt("a_v", [128, 18, 4, 192], BF16) as v_h, sbt("a_k", [96, 2, NT], BF16) as k_h, \
                    sbt("a_q", [96, 2, NT], BF16) as q_h, sbt("a_e", [128, 4, 512], BF16) as e_h, \
                    sbt("a_st", [128, NT], BF16) as st_h:
                Va, kT, qT, Eb, st = v_h.ap(), k_h.ap(), q_h.ap(), e_h.ap(), st_h.ap()
                T.op("pool", lambda e: e.memset(Va[:, :, :, 64:128], 1.0), writes=tuple(VB))
                for kt in range(18):
                    ti = 0 if kt < 2 else 1 + (kt - 2) // 4
                    pm, pmB = psum("mm")
                    T.mm(pmB, pm[:, :], [(ckvn[:, c2, kt * 128:(kt + 1) * 128], wukv[:, 1, c2, :]) for c2 in range(2)],
                         reads=(ckvnB[ti], wB))
                    pv = pm.rearrange("p (a h e) -> p a h e", a=4, h=2)
                    T.op("act", lambda e, pv=pv, kt=kt: e.activation(out=Va[:, kt, :, 0:64], in_=pv[:, :, 0, :], func=AF.Copy),
                         reads=(pmB,), writes=(VB[kt],))
                    T.op("dve", lambda e, pv=pv, kt=kt: e.tensor_copy(out=Va[:, kt, :, 128:192], in_=pv[:, :, 1, :]),
                         reads=(pmB,), writes=(VB[kt],))
                sc = 1.0 / math.sqrt(96.0)
                en = [0]
                LA = 2

                def proj_head(h):
                    hb2 = h % 2
                    for ti, (t0, t1) in enumerate(TILES):
                        w = t1 - t0
                        pm, pmB = psum("p")
                        T.mm(pmB, pm[0:64, :w], [(wukv[:, 0, c2, h * 64:(h + 1) * 64], ckvn[:, c2, t0:t1]) for c2 in range(2)],
                             reads=(wB, ckvnB[ti]))
                        T.op("dve", lambda e, pm=pm: e.tensor_copy(out=kT[0:64, hb2, t0:t1], in_=pm[0:64, :w]), reads=(pmB,), writes=(kB[hb2][ti],))
                        T.op("pool", lambda e: e.tensor_copy(out=kT[64:96, hb2, t0:t1], in_=krT[64:96, t0:t1]), reads=(krB[ti],), writes=(kB[hb2][ti],))
                        if ti == 0 and last:
                            continue
                        pa, paB = psum("p")
                        T.mm(paB, pa[0:96, :w], [(wuq[:, 0, c3, h * 96:(h + 1) * 96], cqn[:, c3, t0:t1]) for c3 in range(3)],
                             reads=(wB, cqnB[ti]))
                        T.op("dve", lambda e, pa=pa: e.tensor_copy(out=qT[0:64, hb2, t0:t1], in_=pa[0:64, :w]), reads=(paB,), writes=(qB[hb2][ti],))
                        if ti == 0:
                            T.op("dve", lambda e, pa=pa: e.tensor_copy(out=qT[64:96, hb2, t0:t1], in_=pa[64:96, :w]), reads=(paB,), writes=(qB[hb2][ti],))
                        else:
                            pb, pbB = psum("p")
                            T.mm(pbB, pb[0:96, :w], [(wuq[:, 1, c3, h * 96:(h + 1) * 96], cqn[:, c3, t0:t1]) for c3 in range(3)],
                                 reads=(wB, cqnB[ti]))
                            rb = load_rope(t0, t1)
                            apply_rope(qT[64:96, hb2, t0:t1], qB[hb2][ti], pa, paB, pb, pbB, rb, w)

                proj_head(0)
                for h in range(NH):
                    pr, od = h // 2, h % 2
                    hb2 = h % 2
                    vsl = slice(0, 128) if od == 0 else slice(64, 192)
                    nsl, dsl = (slice(0, 64), slice(64, 128)) if od == 0 else (slice(64, 128), slice(0, 64))
                    items = []
                    for ti, (t0, t1) in enumerate(TILES):
                        if ti == 0 and last:
                            continue
                        kts = list(range(2)) if ti == 0 else list(range(18))
                        for kt in kts:
                            items.append((ti, kt, kt == kts[0], kt == kts[-1]))
                    sq_ = {}

                    def issue_s(i):
                        ti, kt, _, _ = items[i]
                        t0, t1 = TILES[ti]
                        kti = 0 if kt < 2 else 1 + (kt - 2) // 4
                        pm, pmB = psum("s")
                        T.mm(pmB, pm[:, :t1 - t0], [(kT[0:96, hb2, kt * 128:(kt + 1) * 128], qT[0:96, hb2, t0:t1])], reads=(kB[hb2][kti], qB[hb2][ti]))
                        sq_[i] = (pm, pmB)

                    for i in range(min(LA, len(items))):
                        issue_s(i)
                    po = poB = None
                    mid = len(items) // 2
                    for i, (ti, kt, first, lastk) in enumerate(items):
                        t0, t1 = TILES[ti]
                        w = t1 - t0
                        if i + LA < len(items):
                            issue_s(i + LA)
                        if i == mid and h + 1 < NH:
                            proj_head(h + 1)
                        if first:
                            po, poB = psum("acc")
                        pm, pmB = sq_.pop(i)
                        eb = en[0] % 4
                        en[0] += 1
                        T.op("act", lambda e, pm=pm, eb=eb, w=w: e.activation(out=Eb[:, eb, :w], in_=pm[:, :w], func=AF.Exp, scale=sc),
                             reads=(pmB,), writes=(EB[eb],))
                        for d in T._deps((EB[eb], VB[kt]), (poB,) if first else ()):
                            if not d[0].startswith("pe_"):
                                T._wait("pe", d)
                        ins = nc.tensor.matmul(po[:, :w], lhsT=Va[:, kt, pr, vsl], rhs=Eb[:, eb, :w], start=first, stop=lastk)
                        dep = T._bump("pe")
                        ins.then_inc(dep[1], 1)
                        T._record(dep, (EB[eb], VB[kt]), (poB,))
                        if lastk:
                            rb = od
                            T.op("dve", lambda e, po=po, rb=rb, w=w: e.reciprocal(out=rd[dsl, rb, :w], in_=po[dsl, :w]), reads=(poB,), writes=(rdB[rb],))
                            T.op("dve", lambda e, po=po, rb=rb, w=w, t0=t0, t1=t1: e.tensor_tensor(out=st[nsl, t0:t1], in0=po[nsl, :w], in1=rd[dsl, rb, :w], op=ALU.mult),
                                 reads=(poB, rdB[rb]), writes=(stB[ti],))
                    if od == 1:
                        for ti, (t0, t1) in enumerate(TILES):
                            if ti == 0 and last:
                                continue
                            T.dma_op("sp", attT_d[:, pr, t0:t1], st[:, t0:t1], reads=(stB[ti],), writes=(TB["att"][ti],))
        T.barrier()

    def fourier(l, last):
        with sbt("f_ab", [128, 18, 4, 256], BF16) as ab_h:
            AB = ab_h.ap()
            ABB = [Buf(f"ab{i}") for i in range(18)]
            with sbt("f_w", [128, KD, 512], BF16) as wf_h, sbt("f_h", [128, 2, KD, 512], BF16) as ht_h, \
                    sbt("f_u", [128, 2, 4, 512], BF16) as uf_h, sbt("f_c", [128, 256], BF16) as csc_h:
                wf, ht, uf, csc = wf_h.ap(), ht_h.ap(), uf_h.ap(), csc_h.ap()
                wfB, cscB = Buf("wf"), Buf("csc")
                htB = [Buf("fht0"), Buf("fht1")]
                ufB = [Buf("uf0"), Buf("uf1")]
                load_w(wf, W["w_in_f"][l], wfB)
                T.dma_op("sp", csc, W["dft_c"], writes=(cscB,))
                n = 0
                fseq = [ti for ti in range(len(TILES)) if not (ti == 0 and last)]

                def f_load(ti):
                    t0, t1 = TILES[ti]
                    T.dma_op("sp", ht[:, ti % 2, :, :t1 - t0], hT_d[:, :, t0:t1], reads=(TB["h"][ti],), writes=(htB[ti % 2],))

                f_load(fseq[0])
                for fi, ti in enumerate(fseq):
                    t0, t1 = TILES[ti]
                    w = t1 - t0
                    hb = ti % 2
                    if fi + 1 < len(fseq):
                        f_load(fseq[fi + 1])
                    for g in range(4):
                        pm, pmB = psum("mm")
                        T.mm(pmB, pm[:, :w], [(wf[:, k, g * 128:(g + 1) * 128], ht[:, hb, k, :w]) for k in range(KD)], reads=(wfB, htB[hb]))
                        if g % 2 == 0:
                            T.op("act", lambda e, pm=pm, g=g: e.activation(out=uf[:, hb, g, :w], in_=pm[:, :w], func=AF.Copy), reads=(pmB,), writes=(ufB[hb],))
                        else:
                            T.op("dve", lambda e, pm=pm, g=g: e.tensor_copy(out=uf[:, hb, g, :w], in_=pm[:, :w]), reads=(pmB,), writes=(ufB[hb],))
                    for sub in range(w // 128):
                        kt = t0 // 128 + sub
                        for gp in range(2):
                            pm, pmB = psum("mm")
                            for g2 in range(2):
                                g = gp * 2 + g2
                                T.mm(pmB, pm[:, g2 * 256:(g2 + 1) * 256], [(uf[:, hb, g, sub * 128:(sub + 1) * 128], csc)], reads=(ufB[hb], cscB))
                            dst = AB[:, kt, gp * 2:(gp + 1) * 2, :].rearrange("p a b -> p (a b)")
                            if n % 2 == 0:
                                T.op("act", lambda e, pm=pm, dst=dst: e.activation(out=dst, in_=pm, func=AF.Copy), reads=(pmB,), writes=(ABB[kt],))
                            else:
                                T.op("dve", lambda e, pm=pm, dst=dst: e.tensor_copy(out=dst, in_=pm), reads=(pmB,), writes=(ABB[kt],))
                            n += 1
            T.barrier()
            with sbt("f_tl", [128, 2, 2, 16, 512], BF16) as tl_h, sbt("f_tc", [128, 2, 2, 256], BF16) as tc_h, \
                    sbt("f_y", [128, 2, 4, 512], BF16) as y_h:
                tl, tcx, yst = tl_h.ap(), tc_h.ap(), y_h.ap()
                tlB = [Buf("tl0"), Buf("tl1")]
                tcB = Buf("tcx")
                yB = [Buf("y0"), Buf("y1")]
                n = 0
                if not last:
                    T.dma_op("sp", tcx[:, 0], W["dft_cc"], writes=(tcB,))
                    T.dma_op("sp", tcx[:, 1], W["dft_cs"], writes=(tcB,))
                    yb = n % 2
                    n += 1
                    for g in range(4):
                        pm, pmB = psum("mm")
                        T.mm(pmB, pm[:, :C], [(AB[:, j, g, 0:128], tcx[:, 0, j, :]) for j in range(2)] +
                             [(AB[:, j, g, 128:256], tcx[:, 1, j, :]) for j in range(2)], reads=(ABB[0], ABB[1], tcB))
                        T.op("act", lambda e, pm=pm, g=g: e.activation(out=yst[:, yb, g, :C], in_=pm[:, :C], func=AF.Copy, scale=1.0 / math.sqrt(C * 128.0)),
                             reads=(pmB,), writes=(yB[yb],))
                    T.dma_op("sp", yfT_d[:, :, 0:C], yst[:, yb, :, :C], reads=(yB[yb],), writes=(TB["yf"][0],))
                def t_load(ot):
                    for hf_ in range(2):
                        js = slice(hf_ * 8, (hf_ + 1) * 8)
                        T.dma_op("sp", tl[:, ot % 2, 0, js], W["dft_lc"][:, js, ot * 512:(ot + 1) * 512], writes=(tlB[ot % 2],))
                        T.dma_op("sp", tl[:, ot % 2, 1, js], W["dft_ls"][:, js, ot * 512:(ot + 1) * 512], writes=(tlB[ot % 2],))

                t_load(0)
                for ot in range(4):
                    tb = ot % 2
                    if ot + 1 < 4:
                        t_load(ot + 1)
                    yb = n % 2
                    n += 1
                    for g in range(4):
                        pm, pmB = psum("mm")
                        T.mm(pmB, pm, [(AB[:, 2 + j, g, 0:128], tl[:, tb, 0, j, :]) for j in range(16)] +
                             [(AB[:, 2 + j, g, 128:256], tl[:, tb, 1, j, :]) for j in range(16)], reads=tuple(ABB[2:]) + (tlB[tb],))
                        if g % 2 == 0:
                            T.op("act", lambda e, pm=pm, g=g: e.activation(out=yst[:, yb, g, :], in_=pm, func=AF.Copy, scale=1.0 / 512.0), reads=(pmB,), writes=(yB[yb],))
                        else:
                            T.op("dve", lambda e, pm=pm, g=g: e.tensor_scalar(out=yst[:, yb, g, :], in0=pm, scalar1=1.0 / 512.0, scalar2=None, op0=ALU.mult), reads=(pmB,), writes=(yB[yb],))
                    T.dma_op("sp", yfT_d[:, :, C + ot * 512:C + (ot + 1) * 512], yst[:, yb], reads=(yB[yb],), writes=(TB["yf"][1 + ot],))
        T.barrier()

    def rnn(l, last):
        RT = [(0, C, 0, C)] + [(C + i * 410, min(C + (i + 1) * 410, NT), C, NT) for i in range(5)]
        with sbt("r_w", [128, 2, KD, 512], BF16) as w_h, sbt("r_rgw", [128, 4, 4, 128], BF16) as rgw_h, \
                sbt("r_h", [128, 2, KD, 416], BF16) as ht_h, sbt("r_gug", [128, 2, NT], BF16) as gug_h, \
                sbt("r_uc", [128, 2, NT], F32) as uc_h, sbt("r_ucb", [128, 2, NT], BF16) as ucb_h, \
                sbt("r_G", [128, 2, NT], F32) as G_h, sbt("r_sc", [128, NT], F32) as sc_h, sbt("r_hf", [128, NT], F32) as hf_h, \
                sbt("r_hb", [128, NT], F32) as hb_h, sbt("r_gb1", [128, NT], F32) as hg_h:
            wrg, rgw, ht, gug, uc, ucb = w_h.ap(), rgw_h.ap(), ht_h.ap(), gug_h.ap(), uc_h.ap(), ucb_h.ap()
            G, sc_, hf, hbk, gb1 = G_h.ap(), sc_h.ap(), hf_h.ap(), hb_h.ap(), hg_h.ap()
            A1B = [Buf(f"A1_{i}") for i in range(5)]
            B1B = [Buf(f"B1_{i}") for i in range(5)]
            wB = [Buf("rw0"), Buf("rw1")]
            rgwB = Buf("rgw")
            htB = [Buf("rht0"), Buf("rht1")]
            gugB = [[Buf(f"gug{a_}_{i}") for i in range(6)] for a_ in range(2)]
            ucT = [[Buf(f"uc{a_}_{i}") for i in range(6)] for a_ in range(2)]
            ucbT = [[Buf(f"ucb{a_}_{i}") for i in range(6)] for a_ in range(2)]
            G0B = [Buf(f"G0_{i}") for i in range(5)]
            G1B = [Buf(f"G1_{i}") for i in range(5)]
            scB = [Buf(f"sc_{i}") for i in range(5)]
            hfB = Buf("hf")
            seqA = [(c, ri) for c in range(4) for ri in range(6)]

            def rng(ri):
                o0, o1, lo, hi = RT[ri]
                return o0, o1, lo, hi, max(o0 - 2, lo), min(o1 + 1, hi)

            def a_hload(n):
                c, ri = seqA[n]
                o0, o1, lo, hi, c0, c1 = rng(ri)
                tis = [i for i, (a_, b_) in enumerate(TILES) if a_ < c1 and b_ > c0]
                T.dma_op("sp", ht[:, n % 2, :, :c1 - c0], hT_d[:, :, c0:c1], reads=tuple(TB["h"][i] for i in tis), writes=(htB[n % 2],))

            def a_wload(c):
                if c == 0:
                    load_w(wrg[:, 0], W["w_in_r"][l], wB[0])
                    load_w(wrg[:, 1], W["w_in_g"][l], wB[0])

            def stageA(c):
                cb = c % 2
                cw = lambda k: cst["rnn_cw"][:, l, k, c:c + 1]
                for ri in range(6):
                    n = c * 6 + ri
                    if n + 1 < len(seqA):
                        a_hload(n + 1)
                    hb_ = n % 2
                    o0, o1, lo, hi, c0, c1 = rng(ri)
                    ncp, no, off = c1 - c0, o1 - o0, o0 - c0
                    pu, puB = psum("mm")
                    T.mm(puB, pu[:, :ncp], [(wrg[:, 0, k, c * 128:(c + 1) * 128], ht[:, hb_, k, :ncp]) for k in range(KD)], reads=(wB[0], htB[hb_]))
                    pg, pgB = psum("mm")
                    T.mm(pgB, pg[:, :ncp], [(wrg[:, 1, k, c * 128:(c + 1) * 128], ht[:, hb_, k, :ncp]) for k in range(KD)], reads=(wB[0], htB[hb_]))
                    T.op("act", lambda e, pu=pu: e.activation(out=uc[:, cb, o0:o1], in_=pu[:, off:off + no], func=AF.Identity,
                                                           scale=cw(2), bias=cst["rnn_cb"][:, l, c:c + 1]), reads=(puB, cstB), writes=(ucT[cb][ri],))
                    for k, sh in ((0, -2), (1, -1), (3, 1)):
                        ta = max(o0, lo - sh) if sh < 0 else o0
                        tb = o1 if sh < 0 else min(o1, hi - sh)
                        T.op("dve", lambda e, pu=pu, k=k, sh=sh, ta=ta, tb=tb: e.scalar_tensor_tensor(
                            out=uc[:, cb, ta:tb], in0=pu[:, ta + sh - c0:tb + sh - c0], scalar=cw(k), in1=uc[:, cb, ta:tb],
                            op0=ALU.mult, op1=ALU.add), reads=(puB, cstB, ucT[cb][ri]), writes=(ucT[cb][ri],))
                    T.op("act", lambda e, pg=pg: e.activation(out=gug[:, cb, o0:o1], in_=pg[:, off:off + no], func=AF.Gelu_apprx_tanh),
                         reads=(pgB,), writes=(gugB[cb][ri],))
                    T.op("pool", lambda e: e.tensor_copy(out=ucb[:, cb, o0:o1], in_=uc[:, cb, o0:o1]), reads=(ucT[cb][ri],), writes=(ucbT[cb][ri],))

            def stageB(c):
                cb = c % 2
                for d in range(2):
                    if d == 0:
                        gA = lambda t0, t1: G[:, 0, t0:t1]
                        gB = lambda t0, t1: G[:, 1, t0:t1]
                        AB_, BB_ = G0B, G1B
                    else:
                        gA = lambda t0, t1: hbk[:, t0:t1]
                        gB = lambda t0, t1: gb1[:, t0:t1]
                        AB_, BB_ = A1B, B1B
                    for ti, (t0, t1) in enumerate(TILES):
                        w = t1 - t0
                        for gi_, (gX, XB_, hbt) in enumerate(((gA, AB_, hba_t), (gB, BB_, hbx_t))):
                            pm, pmB = psum("mm")
                            T.mm(pmB, pm[:, :w], [(rgw[:, c, 2 * d + gi_, :], ucb[:, cb, t0:t1])], reads=(rgwB,) + tuple(ucbT[cb]))
                            T.op("act", lambda e, pm=pm, gX=gX, hbt=hbt: e.activation(out=gX(t0, t1), in_=pm[:, :w], func=AF.Tanh, scale=0.5,
                                                                                   bias=hbt[:, l, d, c:c + 1]), reads=(pmB, cstB), writes=(XB_[ti],))
                        T.op("act", lambda e: e.activation(out=gA(t0, t1), in_=gA(t0, t1), func=AF.Exp, scale=hcl_t[:, l, d, c:c + 1],
                                                           bias=hcl_t[:, l, d, c:c + 1]), reads=(AB_[ti], cstB), writes=(AB_[ti],))
                        T.op("pool", lambda e: e.tensor_tensor(out=sc_[:, t0:t1], in0=gA(t0, t1), in1=gA(t0, t1), op=ALU.mult),
                             reads=(AB_[ti],), writes=(scB[ti],))
                        T.op("pool", lambda e: e.tensor_scalar(out=sc_[:, t0:t1], in0=sc_[:, t0:t1], scalar1=1.0, scalar2=0.0, op0=ALU.min, op1=ALU.max),
                             reads=(scB[ti],), writes=(scB[ti],))
                        T.op("dve", lambda e: e.scalar_tensor_tensor(out=gB(t0, t1), in0=gB(t0, t1), scalar=1.0, in1=uc[:, cb, t0:t1],
                                                                     op0=ALU.add, op1=ALU.mult), reads=(BB_[ti],) + tuple(ucT[cb]), writes=(BB_[ti],))
                    for ti, (t0, t1) in enumerate(TILES):
                        T.op("act", lambda e: e.activation(out=sc_[:, t0:t1], in_=sc_[:, t0:t1], func=AF.Sqrt, scale=-1.0, bias=1.0),
                             reads=(scB[ti],), writes=(scB[ti],))
                        T.op("dve", lambda e: e.scalar_tensor_tensor(out=gB(t0, t1), in0=gB(t0, t1), scalar=0.5, in1=sc_[:, t0:t1],
                                                                     op0=ALU.mult, op1=ALU.mult), reads=(BB_[ti], scB[ti]), writes=(BB_[ti],))
                    if d == 0:
                        T.op("dve", lambda e: e.tensor_tensor_scan(out=hf, data0=G[:, 0, :], data1=G[:, 1, :], initial=0.0, op0=ALU.mult, op1=ALU.add),
                             reads=tuple(G0B) + tuple(G1B), writes=(hfB,))
                    else:
                        T.op("dve", lambda e: e.tensor_tensor_scan(out=G[:, 0, 0:C][:, ::-1], data0=hbk[:, 0:C][:, ::-1], data1=gb1[:, 0:C][:, ::-1],
                                                                   initial=0.0, op0=ALU.mult, op1=ALU.add), reads=(A1B[0], B1B[0]), writes=tuple(G0B))
                        T.op("dve", lambda e: e.tensor_tensor_scan(out=G[:, 0, C:NT][:, ::-1], data0=hbk[:, C:NT][:, ::-1], data1=gb1[:, C:NT][:, ::-1],
                                                                   initial=G[:, 0, 0:1], op0=ALU.mult, op1=ALU.add),
                             reads=tuple(A1B) + tuple(B1B) + tuple(G0B), writes=tuple(G0B))
                T.op("dve", lambda e: e.tensor_tensor(out=hf, in0=hf, in1=G[:, 0, :], op=ALU.add), reads=(hfB,) + tuple(G0B), writes=(hfB,))
                T.op("dve", lambda e: e.tensor_tensor(out=ucb[:, cb, :], in0=hf, in1=gug[:, cb, :], op=ALU.mult),
                     reads=(hfB,) + tuple(gugB[cb]), writes=tuple(ucbT[cb]))
                T.dma_op("sp", hgT_d[:, c, :], ucb[:, cb, :], reads=tuple(ucbT[cb]), writes=tuple(TB["hg"]))

            a_hload(0)
            a_wload(0)
            a_wload(1)
            load_w(rgw, W["rg_w"][l], rgwB)
            stageA(0)
            for c in range(4):
                if c + 1 < 4:
                    if c + 2 < 4:
                        pass
                    stageA(c + 1)
                    if c + 2 < 4:
                        a_wload(c + 2)
                stageB(c)
        T.barrier()

    def merge(l, last):
        with sbt("m_wo", [128, 2, 3, 4, 256], BF16) as wo_h, sbt("m_wgl", [128, 2, KD, 3, 256], BF16) as wgl_h, \
                sbt("m_h", [128, 3, KD, 512], BF16) as ht_h, sbt("m_br", [128, 3, 3, 4, 512], BF16) as br_h, \
                sbt("m_g", [128, 2, 512], F32) as g_h, sbt("m_acc", [128, 2, 512], F32) as acc_h, \
                sbt("m_tmp", [128, 2, 512], F32) as tmp_h, sbt("m_st", [128, 2, 2, 512], BF16) as st_h:
            wo, wgl, ht, brt, gs, acc, tmp, mst = wo_h.ap(), wgl_h.ap(), ht_h.ap(), br_h.ap(), g_h.ap(), acc_h.ap(), tmp_h.ap(), st_h.ap()
            wB = [Buf("mw0"), Buf("mw1")]
            htB = [Buf("mht0"), Buf("mht1"), Buf("mht2")]
            brB = [Buf("mbr0"), Buf("mbr1"), Buf("mbr2")]
            gB = [Buf("mg0"), Buf("mg1")]
            accB = [Buf("macc0"), Buf("macc1")]
            tmpB = [Buf("mtmp0"), Buf("mtmp1")]
            stB = [Buf("mst0"), Buf("mst1")]
            srcs = ((attT_d, "att"), (yfT_d, "yf"), (hgT_d, "hg"))
            wnames = ("w_o_attn", "w_o_fourier", "w_o_rnn")
            gn = 0
            NQ = 4
            seq = [(qd, ti) for qd in range(NQ) for ti in range(len(TILES)) if not (ti == 0 and last)]

            def m_load(n):
                qd, ti = seq[n]
                t0, t1 = TILES[ti]
                w = t1 - t0
                b = n % 3
                T.dma_op("sp", ht[:, b, :, :w], hT_d[:, :, t0:t1], reads=(TB["h"][ti],), writes=(htB[b],))
                for br, (src, nm) in enumerate(srcs):
                    T.dma_op("sp", brt[:, b, br, :, :w], src[:, :, t0:t1], reads=(TB[nm][ti],), writes=(brB[b],))

            def m_wload(qd):
                qs = slice(qd * 256, (qd + 1) * 256)
                for br in range(3):
                    load_w(wo[:, qd % 2, br], W[wnames[br]][l][:, :, qs], wB[qd % 2])
                    load_w(wgl[:, qd % 2, :, br, :], W["w_in_gl"][l][:, :, br * 1024 + qd * 256:br * 1024 + (qd + 1) * 256], wB[qd % 2])

            m_load(0)
            m_wload(0)
            if len(seq) > 1:
                m_load(1)
            for n, (qd, ti) in enumerate(seq):
                t0, t1 = TILES[ti]
                w = t1 - t0
                b = n % 3
                sbi = n % 2
                wq = qd % 2
                if (n == 0 or seq[n - 1][0] != qd) and qd + 1 < NQ:
                    m_wload(qd + 1)
                if n + 2 < len(seq):
                    m_load(n + 2)
                for oc2 in range(2):
                    ab = oc2 % 2
                    for br in range(3):
                        py, pyB = psum("mm")
                        T.mm(pyB, py[:, :w], [(wo[:, wq, br, k, oc2 * 128:(oc2 + 1) * 128], brt[:, b, br, k, :w]) for k in range(4)], reads=(wB[wq], brB[b]))
                        pg, pgB = psum("mm")
                        T.mm(pgB, pg[:, :w], [(wgl[:, wq, k, br, oc2 * 128:(oc2 + 1) * 128], ht[:, b, k, :w]) for k in range(KD)], reads=(wB[wq], htB[b]))
                        gb = gn % 2
                        gn += 1
                        T.op("act", lambda e, pg=pg, gb=gb: e.activation(out=gs[:, gb, :w], in_=pg[:, :w], func=AF.Sigmoid), reads=(pgB,), writes=(gB[gb],))
                        if br == 0:
                            T.op("dve", lambda e, py=py, gb=gb, ab=ab: e.tensor_tensor(out=acc[:, ab, :w], in0=py[:, :w], in1=gs[:, gb, :w], op=ALU.mult),
                                 reads=(pyB, gB[gb]), writes=(accB[ab],))
                        else:
                            T.op("dve", lambda e, py=py, gb=gb: e.tensor_tensor(out=tmp[:, gb, :w], in0=py[:, :w], in1=gs[:, gb, :w], op=ALU.mult),
                                 reads=(pyB, gB[gb]), writes=(tmpB[gb],))
                            if br == 1:
                                T.op("dve", lambda e, gb=gb, ab=ab: e.tensor_tensor(out=acc[:, ab, :w], in0=acc[:, ab, :w], in1=tmp[:, gb, :w], op=ALU.add),
                                     reads=(accB[ab], tmpB[gb]), writes=(accB[ab],))
                            else:
                                T.op("dve", lambda e, gb=gb, ab=ab, oc2=oc2: e.tensor_tensor(out=mst[:, sbi, oc2, :w], in0=acc[:, ab, :w], in1=tmp[:, gb, :w], op=ALU.add),
                                     reads=(accB[ab], tmpB[gb]), writes=(stB[sbi],))
                T.dma_op("sp", mT_d[:, qd * 2:(qd + 1) * 2, t0:t1], mst[:, sbi, :, :w], reads=(stB[sbi],), writes=(TB["m"][ti],))
        T.barrier()

    def merge2_and_norm(l, last):
        with sbt("m_wout", [128, KD, D], BF16) as wout_h, sbt("m_m", [128, 2, KD, 512], BF16) as mt_h:
            wout, mt = wout_h.ap(), mt_h.ap()
            woB = Buf("wout")
            mtB = [Buf("mt0"), Buf("mt1")]
            load_w(wout, W["w_out"][l], woB)
            seq2 = [ti for ti in range(len(TILES)) if not (ti == 0 and last)]

            def m2_load(n):
                ti = seq2[n]
                t0, t1 = TILES[ti]
                T.dma_op("sp", mt[:, n % 2, :, :t1 - t0], mT_d[:, :, t0:t1], reads=(TB["m"][ti],), writes=(mtB[n % 2],))

            m2_load(0)

            def pre(ti):
                n = seq2.index(ti)
                t0, t1 = TILES[ti]
                w = t1 - t0
                col = 1 if ti == 0 else 0
                b = n % 2
                if n + 1 < len(seq2):
                    m2_load(n + 1)
                for oc in range(KD):
                    pm, pmB = psum("mm")
                    T.mm(pmB, pm[:, :w], [(wout[:, k, oc * 128:(oc + 1) * 128], mt[:, b, k, :w]) for k in range(KD)], reads=(woB, mtB[b]))
                    T.op("dve", lambda e, pm=pm, oc=oc: e.scalar_tensor_tensor(out=xT[:, oc, t0:t1], in0=pm[:, :w], scalar=lay2[:, l % 2, 2, oc, col:col + 1],
                                                                           in1=xT[:, oc, t0:t1], op0=ALU.mult, op1=ALU.add),
                         reads=(pmB, layB2[l % 2], XB[ti]), writes=(XB[ti],))

            norm_modulate(l, 1, h2T_d, TB["h2"], pre=pre, tiles=seq2)
        T.barrier()

    def ffn(l, last):
        units = []
        if last:
            units.append([(256, 766, 256, NT)])
        else:
            units.append([(0, 256, 0, 256), (256, 766, 256, NT)])
        units.append([(766, 1534, 256, NT)])
        units.append([(1534, 2304, 256, NT)])
        AW = 772
        with sbt("f_a", [128, NJ, 770], BF16) as a_h, sbt("f_h2", [128, KD, AW], BF16) as h2_h, \
                sbt("f_wup", [128, 3, KD, 256], BF16) as wup_h, \
                sbt("f_cu", [128, 2, 2, 770], F32) as cu_h, sbt("f_wd", [128, 2, NJ, 128], BF16) as wd_h, \
                sbt("f_wada", [128, 2, KD, 512], BF16) as wada_h:
            a_sb, h2u, wup, cu, wd, wada = a_h.ap(), h2_h.ap(), wup_h.ap(), cu_h.ap(), wd_h.ap(), wada_h.ap()
            wadaB = [Buf("fwada0"), Buf("fwada1")]
            ada_t = [0]

            def ada_tick():
                if l + 1 >= n_layers:
                    return
                t = ada_t[0]
                ada_t[0] += 1
                if t < 12:
                    T.dma_op("pool", wada[:, t % 2], W["w_ada"][l + 1, :, :, t * 512:(t + 1) * 512], writes=(wadaB[t % 2],))
                g = t - 1
                if 0 <= g < 12:
                    pm, pmB = psum("all")
                    for ff in range(4):
                        T.mm(pmB, pm[:, 2 * ff:2 * ff + 2],
                             [(wada[:, g % 2, k, ff * 128:(ff + 1) * 128], silu_c[:, k, :]) for k in range(KD)],
                             reads=(wadaB[g % 2], cstB))
                    T.op("dve", lambda e, pm=pm, g=g: e.tensor_tensor(
                        out=mod[:, l + 1, g * 4:(g + 1) * 4, :], in0=pm[:, 0:8].rearrange("p (f c) -> p f c", c=2),
                        in1=cst["b_ada"][:, l + 1, g * 4:(g + 1) * 4].unsqueeze(2).to_broadcast([128, 4, 2]), op=ALU.add),
                        reads=(pmB, cstB), writes=(modB,))

            aB = [Buf(f"a{j}") for j in range(NJ)]
            h2B = Buf("h2u")
            wupB = [Buf("wup0"), Buf("wup1"), Buf("wup2")]
            cuB = [[[Buf(f"cu{a_}_{g_}_{o_}") for o_ in range(3)] for g_ in range(2)] for a_ in range(2)]
            wdB = [Buf("wd0"), Buf("wd1")]
            wn = 0
            pend_norm = [None]
            for unit in units:
                otl = []
                pos = 0
                q = 0
                for (o0, o1, lo, hi) in unit:
                    sc0, sc1 = max(o0 - 1, lo), min(o1 + 1, hi)
                    tis = [i for i, (a_, b_) in enumerate(TILES) if a_ < sc1 and b_ > sc0]
                    T.dma_op("sp", h2u[:, :, pos:pos + sc1 - sc0], h2T_d[:, :, sc0:sc1], reads=tuple(TB["h2"][i] for i in tis), writes=(h2B,))
                    nt_ = (o1 - o0 + 509) // 510
                    tw = (o1 - o0 + nt_ - 1) // nt_
                    for a0 in range(o0, o1, tw):
                        a1 = min(o1, a0 + tw)
                        c0, c1 = max(a0 - 1, lo), min(a1 + 1, hi)
                        otl.append((a0, a1, c0, c1, pos + c0 - sc0, q))
                        q += a1 - a0
                    pos += sc1 - sc0
                nout = q

                def load_up(j):
                    load_w(wup[:, j % 3], W["w_up"][l, j], wupB[j % 3])

                load_up(0)
                load_up(1)
                if pend_norm[0] is not None:
                    pend_norm[0]()
                    pend_norm[0] = None

                def load_dn(oc, wb):
                    load_w(wd[:, wb], W["w_down"][l, oc], wdB[wb])

                banks = {}
                for j in range(NJ + 1):
                    if len(otl) <= 2:
                        ada_tick()
                    if j == NJ - 4:
                        load_dn(0, wn % 2)
                    if j < NJ and j + 2 < NJ:
                        load_up(j + 2)
                    jb3 = j % 3
                    jj = j - 1
                    jb = jj % 2
                    for oi, (a0, a1, c0, c1, hp, q0) in enumerate(otl):
                        if j < NJ:
                            for g in range(2):
                                pm, pmB = psum("all")
                                T.mm(pmB, pm[:, :c1 - c0], [(wup[:, jb3, k, g * 128:(g + 1) * 128], h2u[:, k, hp:hp + c1 - c0]) for k in range(KD)],
                                     reads=(wupB[jb3], h2B))
                                banks[(j, oi, g)] = (pm, pmB)
                        if j >= 1:
                            no = a1 - a0
                            b0 = a0 - c0
                            lo_ = 0 if c0 < a0 else 1
                            hi_ = no if c1 > a1 else no - 1
                            for g in range(2):
                                pm, pmB = banks.pop((jj, oi, g))
                                cwp = lambda k, g=g, jj=jj: cst["ffn_cw"][:, l, jj, g, k:k + 1]
                                T.op("act", lambda e, pm=pm, g=g, jj=jj, jb=jb, b0=b0, no=no, q0=q0: e.activation(
                                    out=cu[:, jb, g, q0:q0 + no], in_=pm[:, b0:b0 + no], func=AF.Identity,
                                    scale=cwp(1), bias=cst["ffn_cb"][:, l, jj, g:g + 1]), reads=(pmB, cstB), writes=(cuB[jb][g][oi],))
                                T.op("dve", lambda e, pm=pm, g=g, jb=jb, b0=b0, no=no, q0=q0, lo_=lo_: e.scalar_tensor_tensor(
                                    out=cu[:, jb, g, q0 + lo_:q0 + no], in0=pm[:, b0 + lo_ - 1:b0 + no - 1], scalar=cwp(0),
                                    in1=cu[:, jb, g, q0 + lo_:q0 + no], op0=ALU.mult, op1=ALU.add), reads=(pmB, cstB, cuB[jb][g][oi]), writes=(cuB[jb][g][oi],))
                                T.op("dve", lambda e, pm=pm, g=g, jb=jb, b0=b0, no=no, q0=q0, hi_=hi_: e.scalar_tensor_tensor(
                                    out=cu[:, jb, g, q0:q0 + hi_], in0=pm[:, b0 + 1:b0 + hi_ + 1], scalar=cwp(2),
                                    in1=cu[:, jb, g, q0:q0 + hi_], op0=ALU.mult, op1=ALU.add), reads=(pmB, cstB, cuB[jb][g][oi]), writes=(cuB[jb][g][oi],))
                    if j >= 1:
                        T.op("act", lambda e, jb=jb: e.activation(out=cu[:, jb, 1, :nout], in_=cu[:, jb, 1, :nout], func=AF.Gelu_apprx_tanh),
                             reads=tuple(cuB[jb][1]), writes=tuple(cuB[jb][1]))
                        T.op("pool", lambda e, jb=jb, jj=jj: e.tensor_tensor(out=a_sb[:, jj, :nout], in0=cu[:, jb, 0, :nout], in1=cu[:, jb, 1, :nout], op=ALU.mult),
                             reads=tuple(cuB[jb][0]) + tuple(cuB[jb][1]), writes=(aB[jj],))
                for oc in range(KD):
                    wb = wn % 2
                    wn += 1
                    if oc + 1 < KD:
                        load_dn(oc + 1, wn % 2)
                    for (a0, a1, c0, c1, hp, q0) in otl:
                        tis = [i for i, (a_, b_) in enumerate(TILES) if a_ < a1 and b_ > a0]
                        col = 1 if tis[0] == 0 else 0
                        pm, pmB = psum("all")
                        T.mm(pmB, pm[:, :a1 - a0], [(wd[:, wb, j, :], a_sb[:, j, q0:q0 + a1 - a0]) for j in range(NJ)],
                             reads=(wdB[wb],) + tuple(aB))
                        T.op("dve", lambda e, pm=pm, oc=oc, a0=a0, a1=a1, col=col: e.scalar_tensor_tensor(
                            out=xT[:, oc, a0:a1], in0=pm[:, :a1 - a0], scalar=lay2[:, l % 2, 5, oc, col:col + 1], in1=xT[:, oc, a0:a1],
                            op0=ALU.mult, op1=ALU.add), reads=(pmB, layB2[l % 2]) + tuple(XB[i] for i in tis), writes=tuple(XB[i] for i in tis))
                if l + 1 < n_layers:
                    def _nm(ui=units.index(unit)):
                        if ui == 0:
                            while ada_t[0] < 13:
                                ada_tick()
                            layer_consts(l + 1)
                            norm_modulate(l + 1, 0, hT_d, TB["h"], rngs=[(0, 256), (256, 766)])
                        elif ui == 1:
                            norm_modulate(l + 1, 0, hT_d, TB["h"], rngs=[(766, 1278), (1278, 1534)])
                        else:
                            norm_modulate(l + 1, 0, hT_d, TB["h"], rngs=[(1534, 2046), (2046, 2304)])
                    pend_norm[0] = _nm
            if pend_norm[0] is not None:
                pend_norm[0]()
                pend_norm[0] = None
        T.barrier()

    def final_norm():
        with sbt("fn_sq", [128, 2, 512], BF16) as sq_h, sbt("fn_rs", [128, 2, 512], F32) as rs_h, \
                sbt("fn_o", [128, 2, KD, 512], F32) as o_h:
            sq, rs, ost = sq_h.ap(), rs_h.ap(), o_h.ap()
            sqB = [Buf("fsq0"), Buf("fsq1")]
            rsB = [Buf("frs0"), Buf("frs1")]
            oB = [Buf("fo0"), Buf("fo1")]
            n = 0
            for ti, (t0, t1) in enumerate(TILES):
                if ti == 0:
                    continue
                w = t1 - t0
                pm, pmB = psum("mm")
                for k in range(KD):
                    b = n % 2
                    n += 1
                    T.op("act", lambda e, k=k, b=b: e.activation(out=sq[:, b, :w], in_=xT[:, k, t0:t1], func=AF.Square), reads=(XB[ti],), writes=(sqB[b],))
                    for d in T._deps((sqB[b], cstB), (pmB,) if k == 0 else ()):
                        if not d[0].startswith("pe_"):
                            T._wait("pe", d)
                    ins = nc.tensor.matmul(pm[:, :w], lhsT=ones_bf, rhs=sq[:, b, :w], start=(k == 0), stop=(k == KD - 1))
                    dep = T._bump("pe")
                    ins.then_inc(dep[1], 1)
                    T._record(dep, (sqB[b], cstB), (pmB,))
                rb = ti % 2
                T.op("act", lambda e: e.activation(out=rs[:, rb, :w], in_=pm[:, :w], func=AF.Ln, scale=1.0 / D, bias=eps_t[:, 0:1]), reads=(pmB, cstB), writes=(rsB[rb],))
                T.op("act", lambda e: e.activation(out=rs[:, rb, :w], in_=rs[:, rb, :w], func=AF.Exp, scale=-0.5), reads=(rsB[rb],), writes=(rsB[rb],))
                for k in range(KD):
                    T.op("dve", lambda e, k=k: e.scalar_tensor_tensor(out=ost[:, rb, k, :w], in0=xT[:, k, t0:t1], scalar=cst["final_norm"][:, k:k + 1],
                                                                   in1=rs[:, rb, :w], op0=ALU.mult, op1=ALU.mult), reads=(XB[ti], cstB, rsB[rb]), writes=(oB[rb],))
                T.dma_op("sp", out_T[:, :, t0 - C:t1 - C], ost[:, rb, :, :w], reads=(oB[rb],))

    done_all = False
    for l in range(n_layers):
        last = (l == L - 1)
        if l == 0:
            layer_consts(l)
            norm_modulate(l, 0, hT_d, TB["h"])
            T.barrier()
        if upto == "norm":
            break
        attention(l, last)
        if upto == "attn":
            break
        fourier(l, last)
        if upto == "fourier":
            break
        rnn(l, last)
        if upto == "rnn":
            break
        merge(l, last)
        merge2_and_norm(l, last)
        if upto == "merge":
            break
        ffn(l, last)
        done_all = (l == n_layers - 1)

    if debug:
        for i, (t0, t1) in enumerate(TILES):
            T.dma_op("sp", dbg_x[:, :, t0:t1], xT[:, :, t0:t1], reads=(XB[i],))
    if done_all and n_layers == L:
        final_norm()
    else:
        for i, (t0, t1) in enumerate(TILES[1:]):
            T.dma_op("sp", out_T[:, :, t0 - C:t1 - C], xT[:, :, t0:t1], reads=(XB[i + 1],))
    T.wait_all("sp")
    print(f"[build] instrs={T.n_ins} waits={T.n_wait}")
    return nc


_CACHE = {}


def kernel(**inputs):
    inp = {k: np.asarray(v) for k, v in inputs.items()}
    shared = prep_shared(inp)
    nc = build()
    in_maps = []
    for b in range(8):
        m = dict(shared)
        m.update(prep_core(inp, b))
        in_maps.append(m)
    res = run_bass_kernel_spmd(nc, in_maps, core_ids=list(range(8)))
    out = np.empty((8, S, D), np.float32)
    for b in range(8):
        oT = res.results[b]["out_T"]
        out[b] = oT.transpose(2, 1, 0).reshape(S, D)
    return out
```
